# Optimizing a Trainium2 kernel written in Bass

```python
import jax, jax.numpy as jnp
from jax import lax
import numpy as np

D_MODEL = 1024
BATCH = 8
SEQ = 4096
DEPTH = 2

CHUNK = 64
EPS = 1e-6
GMLP_BLOCK = 128
GMLP_GROUPS = 4
GMLP_GROUP_DIM = 128
GMLP_WIDTH = GMLP_GROUPS * GMLP_GROUP_DIM
HG_HEADS = 8
HG_DK = 128
HG_DV = D_MODEL // HG_HEADS
HG_WIDTH = HG_HEADS * HG_DK
HG_VWIDTH = HG_HEADS * HG_DV
N_BRANCH = 2
D_FF = 4 * D_MODEL
IN_SPLITS = (GMLP_WIDTH, GMLP_WIDTH, HG_WIDTH, HG_WIDTH, HG_VWIDTH, HG_VWIDTH, D_MODEL, D_MODEL)
IN_COLS = sum(IN_SPLITS)

kernel_name = "hybrid_gmlp_hgrn2_gated_trunk"


def rmsnorm(x, g):
    xf = x.astype(jnp.float32)
    y = xf * lax.rsqrt(jnp.mean(xf * xf, axis=-1, keepdims=True) + EPS)
    return (y * g.astype(jnp.float32)).astype(x.dtype)


def layernorm(x, g, b):
    xf = x.astype(jnp.float32)
    mu = jnp.mean(xf, axis=-1, keepdims=True)
    xc = xf - mu
    y = xc * lax.rsqrt(jnp.mean(xc * xc, axis=-1, keepdims=True) + EPS)
    return (y * g.astype(jnp.float32) + b.astype(jnp.float32)).astype(x.dtype)


def chunk_causal_mask(n):
    c = jnp.arange(n) // CHUNK
    return c[None, :] <= c[:, None]


def gmlp_branch(zu, zv, ln_g, ln_b, w_s, b_s):
    B, S, _ = zu.shape
    nb = S // GMLP_BLOCK
    u = jax.nn.gelu(zu, approximate=False)
    v = layernorm(jax.nn.gelu(zv, approximate=False), ln_g, ln_b)
    v = v.reshape(B, nb, GMLP_BLOCK, GMLP_GROUPS, GMLP_GROUP_DIM)
    w = jnp.where(chunk_causal_mask(GMLP_BLOCK)[None], w_s, 0).astype(v.dtype)
    mixed = jnp.einsum('gqp,bnpgc->bnqgc', w, v) + b_s.T.astype(v.dtype)[None, None, :, :, None]
    return u * mixed.reshape(B, S, GMLP_WIDTH)


def hgrn2_lower_bounds(lb_param):
    p = jax.nn.softmax(lb_param.astype(jnp.float32), axis=0)
    return jnp.cumsum(p, axis=0) - p[0]


def hgrn2_chunkwise(q, k, v, log_f):
    B, S, H, DK = q.shape
    DV = v.shape[-1]
    nc = S // CHUNK

    def to_chunks(t):
        return t.reshape(B, nc, CHUNK, H, t.shape[-1]).transpose(1, 0, 3, 2, 4)

    tri = jnp.tril(jnp.ones((CHUNK, CHUNK), dtype=bool))[:, :, None]

    def step(state, inp):
        qb, kb, vb, gb = inp
        cum = jnp.cumsum(gb, axis=2)
        o_inter = jnp.einsum('bhtk,bhkv->bhtv', qb * jnp.exp(cum), state)
        rel = cum[:, :, :, None, :] - cum[:, :, None, :, :]
        decay = jnp.exp(jnp.where(tri, rel, -jnp.inf))
        scores = jnp.einsum('bhtk,bhsk,bhtsk->bhts', qb, kb, decay)
        o_intra = jnp.einsum('bhts,bhsv->bhtv', scores, vb)
        last = cum[:, :, -1:, :]
        k_dec = kb * jnp.exp(last - cum)
        state = state * jnp.exp(last[:, :, 0, :])[..., None] + jnp.einsum('bhsk,bhsv->bhkv', k_dec, vb)
        return state, o_inter + o_intra

    state0 = jnp.zeros((B, H, DK, DV), jnp.float32)
    _, out = lax.scan(step, state0, (to_chunks(q), to_chunks(k), to_chunks(v), to_chunks(log_f)))
    return out.transpose(1, 0, 3, 2, 4).reshape(B, S, H, DV)


def hgrn2_branch(zq, zf, zi, zo, lb, layer_idx, norm_g):
    B, S, _ = zq.shape
    dt = zq.dtype
    zf32 = zf.astype(jnp.float32)
    if layer_idx == 0:
        log_f = jax.nn.log_sigmoid(zf32)
        k = jax.nn.sigmoid(-zf32)
    else:
        f = lb[None, None, :] + (1.0 - lb[None, None, :]) * jax.nn.sigmoid(zf32)
        log_f = jnp.log(f)
        k = 1.0 - f
    q = jax.nn.silu(zq.astype(jnp.float32))
    heads = lambda t, d: t.reshape(B, S, HG_HEADS, d)
    o = hgrn2_chunkwise(heads(q, HG_DK), heads(k, HG_DK),
                        heads(zi.astype(jnp.float32), HG_DV), heads(log_f, HG_DK))
    o = rmsnorm(o, norm_g) * jax.nn.silu(heads(zo.astype(jnp.float32), HG_DV))
    return o.reshape(B, S, HG_VWIDTH).astype(dt)


def setup_inputs(seed: int = 0) -> dict:
    key = jax.random.key(seed)
    ks = jax.random.split(key, 16)
    n = jax.random.normal
    f32 = jnp.float32
    return {
        "x": n(ks[0], (BATCH, SEQ, D_MODEL), f32),
        "w_in": n(ks[1], (DEPTH, D_MODEL, IN_COLS), f32) * D_MODEL ** -0.5,
        "gmlp_ln_g": 1.0 + 0.05 * n(ks[2], (DEPTH, GMLP_WIDTH), f32),
        "gmlp_ln_b": 0.02 * n(ks[3], (DEPTH, GMLP_WIDTH), f32),
        "gmlp_ws": n(ks[4], (DEPTH, GMLP_GROUPS, GMLP_BLOCK, GMLP_BLOCK), f32) * GMLP_BLOCK ** -0.5,
        "gmlp_bs": 1.0 + 0.05 * n(ks[5], (DEPTH, GMLP_GROUPS, GMLP_BLOCK), f32),
        "w_up_a": n(ks[6], (DEPTH, GMLP_WIDTH, D_MODEL), f32) * GMLP_WIDTH ** -0.5,
        "hg_lb": 0.5 * n(ks[7], (DEPTH, HG_WIDTH), f32),
        "hg_norm_g": 1.0 + 0.05 * n(ks[8], (DEPTH, HG_DV), f32),
        "w_up_b": n(ks[9], (DEPTH, HG_VWIDTH, D_MODEL), f32) * HG_VWIDTH ** -0.5,
        "w_out": n(ks[10], (DEPTH, D_MODEL, D_MODEL), f32) * D_MODEL ** -0.5,
        "norm_mix": 1.0 + 0.05 * n(ks[11], (DEPTH, D_MODEL), f32),
        "norm_ffn": 1.0 + 0.05 * n(ks[12], (DEPTH, D_MODEL), f32),
        "w_ff1": n(ks[13], (DEPTH, D_MODEL, D_FF), f32) * D_MODEL ** -0.5,
        "w_ff2": n(ks[14], (DEPTH, D_FF, D_MODEL), f32) * (0.5 * D_FF ** -0.5),
        "final_norm": 1.0 + 0.05 * n(ks[15], (D_MODEL,), f32),
    }


def reference(x, w_in, gmlp_ln_g, gmlp_ln_b, gmlp_ws, gmlp_bs, w_up_a, hg_lb, hg_norm_g,
              w_up_b, w_out, norm_mix, norm_ffn, w_ff1, w_ff2, final_norm):
    split_points = [int(s) for s in np.cumsum(IN_SPLITS)[:-1]]
    lower_bounds = hgrn2_lower_bounds(hg_lb)
    for l in range(DEPTH):
        h = rmsnorm(x, norm_mix[l])
        proj = h @ w_in[l]
        zu, zv, zq, zf, zi, zo, g_a, g_b = jnp.split(proj, split_points, axis=-1)
        y_a = gmlp_branch(zu, zv, gmlp_ln_g[l], gmlp_ln_b[l], gmlp_ws[l], gmlp_bs[l]) @ w_up_a[l]
        y_b = hgrn2_branch(zq, zf, zi, zo, lower_bounds[l], l, hg_norm_g[l]) @ w_up_b[l]
        mix = jax.nn.sigmoid(g_a) * y_a + jax.nn.sigmoid(g_b) * y_b
        x = x + mix @ w_out[l]
        h = rmsnorm(x, norm_ffn[l])
        x = x + jnp.square(jax.nn.relu(h @ w_ff1[l])) @ w_ff2[l]
    return rmsnorm(x, final_norm)
```

```python
from contextlib import ExitStack
import numpy as np
import concourse.bass as bass
import concourse.mybir as mybir
from concourse.bass_utils import run_bass_kernel_spmd

F32 = mybir.dt.float32
BF16 = mybir.dt.bfloat16
AF = mybir.ActivationFunctionType
ALU = mybir.AluOpType

D = 1024
T = 512
NB = 4
EPS = 1e-6
NGRP = 35
NWS = 4
NTMP = 8
G_UPA, G_UPB, G_OUT, G_FF1, G_FF2 = 14, 15, 17, 19, 27


class _Op:
    __slots__ = ("fn", "deps", "signal", "dma_sem", "cnt")


class Prog:
    ENGS = ("pe", "act", "dve", "pool", "sp")

    def __init__(self):
        self.ops = {n: [] for n in self.ENGS}
        self.bufs = {}
        self.dmacnt = {}
        self.bank = 0
        self.tmpi = 0
        self.marks = []
        self.reserved = None

    def mark(self, name):
        self.marks.append((name, len(self.ops["pe"])))

    def newbank(self):
        b = self.bank
        if b == self.reserved:
            b = (b + 1) % 8
        self.bank = (b + 1) % 8
        return b

    def newtmp(self):
        t = self.tmpi
        self.tmpi = (t + 1) % NTMP
        return t

    def _deps(self, R, W):
        deps = []
        for k in R:
            st = self.bufs.get(k)
            if st is not None and st[0] is not None:
                deps.append(st[0])
        for k in W:
            st = self.bufs.get(k)
            if st is not None:
                if st[0] is not None:
                    deps.append(st[0])
                for e, i in st[1].items():
                    deps.append(("e", e, i))
                deps.extend(st[2])
        return deps

    def _record(self, ev, R, W):
        for k in R:
            st = self.bufs.setdefault(k, [None, {}, []])
            if ev[0] == "e":
                st[1][ev[1]] = ev[2]
            else:
                st[2].append(ev)
        for k in W:
            self.bufs[k] = [ev, {}, []]

    def _mk(self, eng, fn, R, W, dma_sem=None):
        o = _Op()
        o.fn = fn
        o.signal = False
        o.dma_sem = dma_sem
        o.cnt = 0
        deps = []
        for d in self._deps(R, W):
            if d[0] == "e":
                if d[1] == eng and eng == "pe":
                    continue
                self.ops[d[1]][d[2]].signal = True
                deps.append(d)
            else:
                deps.append(("d", d[1], self.dmacnt[d[1]]))
        o.deps = deps
        self.ops[eng].append(o)
        return len(self.ops[eng]) - 1

    def op(self, eng, fn, R=(), W=()):
        idx = self._mk(eng, fn, R, W)
        self._record(("e", eng, idx), R, W)

    def dma(self, fn, sem, R=(), W=()):
        self._mk("sp", fn, R, W, dma_sem=sem)
        self.dmacnt[sem] = self.dmacnt.get(sem, 0) + 16
        self._record(("d", sem, self.dmacnt[sem]), R, W)

    def emit(self, nc, block, sems):
        for n in self.ENGS:
            c = 0
            for o in self.ops[n]:
                if o.signal and o.dma_sem is None:
                    c += 1
                o.cnt = c
        ops = self.ops

        def run(engname):
            def body(e):
                waited = {}
                for o in ops[engname]:
                    for d in o.deps:
                        if d[0] == "e":
                            key = d[1]
                            val = ops[d[1]][d[2]].cnt
                        else:
                            key = "dma_" + d[1]
                            val = d[2]
                        if waited.get(key, 0) < val:
                            e.wait_ge(sems[key], val)
                            waited[key] = val
                    ins = o.fn(e)
                    if o.dma_sem is not None:
                        ins.then_inc(sems["dma_" + o.dma_sem], 16)
                    elif o.signal:
                        ins.then_inc(sems[engname], 1)
                if engname == "sp":
                    for s, v in self.dmacnt.items():
                        if waited.get("dma_" + s, 0) < v:
                            e.wait_ge(sems["dma_" + s], v)
            return body

        block.sync(run("sp"))
        block.tensor(run("pe"))
        block.scalar(run("act"))
        block.vector(run("dve"))
        block.gpsimd(run("pool"))


def build_nc(NT, NL=2):
    S = NT * T
    NCH = NT * 8
    nc = bass.Bass("TRN2", target_bir_lowering=False)
    dt = lambda name, shape, dtype=F32, kind="ExternalInput": nc.dram_tensor(name, list(shape), dtype, kind=kind).ap()
    xT = dt("xT", [D, S])
    w_in = dt("w_in", [2, D, 7168])
    w_up_a = dt("w_up_a", [2, 512, D])
    w_up_b = dt("w_up_b", [2, D, D])
    w_out = dt("w_out", [2, D, D])
    w_ff1 = dt("w_ff1", [2, D, 4096])
    w_ff2 = dt("w_ff2", [2, 4096, D])
    gn_d = dt("gn", [128, 40])
    lng_d = dt("lng", [128, 2, 512])
    lnb_d = dt("lnb", [128, 2, 512])
    wst_d = dt("wst", [128, 1024])
    bsr_d = dt("bsr", [1, 1024])
    hglb_d = dt("hglb", [128, 2, 8])
    hgg_d = dt("hgg", [128, 2])
    ident_d = dt("ident", [128, 128])
    maskst_d = dt("maskst", [128, 512])
    scanm_d = dt("scanm", [128, 512])
    yT = dt("yT", [D, S], kind="ExternalOutput")
    wsc = dt("wsc", [2, NGRP, 128, 4096], BF16, kind="Internal")

    P = Prog()
    with ExitStack() as es:
        sb = lambda name, shape, dtype=F32: es.enter_context(nc.sbuf_tensor(name, list(shape), dtype))
        x = sb("x", [128, 8, T])
        hb = sb("hb", [128, 8, T], BF16)
        sq = sb("sq", [128, 2, T], BF16)
        rstd = sb("rstd", [128, T])
        msv = sb("msv", [128, T])
        big = sb("big", [128, 32, T], BF16)
        u = sb("u", [128, 4, T], BF16)
        v = sb("v", [128, 4, T], BF16)
        tmp = sb("tmp", [128, NTMP, T])
        sgr = sb("sgr", [128, 2, T])
        mt = sb("mt", [128, 2, T])
        mixA = sb("mixA", [128, 8, T], BF16)
        ktok = sb("ktok", [128, 1024], BF16)
        PT = sb("PT", [128, 2, 1024], BF16)
        osq = sb("osq", [128, 1024], BF16)
        U = sb("U", [128, 8, 2, 128], BF16)
        msvo = sb("msvo", [128, 2, T])
        rso = sb("rso", [128, 2, T])
        otmp = sb("otmp", [128, 2, T])
        on = sb("on", [128, 8, T], BF16)
        Tm = sb("Tm", [128, 2, 8, 128])
        Et = sb("Et", [128, 2, 8, NCH])
        wsl = sb("wsl", [128, NWS, 4096], BF16)
        identb = sb("identb", [128, 128], BF16)
        onesb = sb("onesb", [128, 128], BF16)
        maskst = sb("maskst_s", [128, 512])
        scanm = sb("scanm_s", [128, 512])
        lng = sb("lng_s", [128, 2, 512])
        lnb = sb("lnb_s", [128, 2, 512])
        wstb = sb("wstb", [128, 2, 4, 128], BF16)
        bsrb = sb("bsrb", [1, 1024], BF16)
        gn = sb("gn_s", [128, 40])
        hgg = sb("hgg_s", [128, 2])
        hglb = sb("hglb_s", [128, 2, 8])
        lb1 = sb("lb1", [128, 8])
        oml = sb("oml", [128, 8])
        lnst = sb("lnst", [128, 2, 8])
        epsc = sb("epsc", [128, 1])
        cf32 = sb("cf32", [128, 6, 512])
        ps = es.enter_context(nc.psum_tensor("ps", [128, 8, 512], F32))

        npar = [0]

        def ld(dst_ap, src_ap, W, R=()):
            P.dma(lambda e: e.dma_start(out=dst_ap, in_=src_ap), "par%d" % npar[0], R=R, W=W)
            npar[0] += 1

        ld(maskst[:], maskst_d[:, :], [("maskst",)])
        ld(scanm[:], scanm_d[:, :], [("scanm",)])
        ld(gn[:], gn_d[:, :], [("gn",)])
        ld(hgg[:], hgg_d[:, :], [("hgg",)])
        ld(hglb[:], hglb_d[:, :, :], [("hglb",)])
        ld(lng[:], lng_d[:, :, :], [("lng",)])
        ld(lnb[:], lnb_d[:, :, :], [("lnb",)])
        t0 = P.newtmp()
        ld(tmp[:, t0, 0:128], ident_d[:, :], [("tmp", t0)])
        P.op("dve", lambda e: e.tensor_copy(out=identb[:], in_=tmp[:, t0, 0:128]), R=[("tmp", t0)], W=[("identb",)])
        t1 = P.newtmp()
        t2 = P.newtmp()
        ld(tmp[:, t1, :], wst_d[:, 0:512], [("tmp", t1)])
        ld(tmp[:, t2, :], wst_d[:, 512:1024], [("tmp", t2)])
        P.op("dve", lambda e: e.tensor_copy(out=wstb[:, 0, :, :], in_=tmp[:, t1, :].rearrange("p (g q) -> p g q", g=4)),
             R=[("tmp", t1)], W=[("wstb",)])
        P.op("dve", lambda e: e.tensor_copy(out=wstb[:, 1, :, :], in_=tmp[:, t2, :].rearrange("p (g q) -> p g q", g=4)),
             R=[("tmp", t2)], W=[("wstb",)])
        for l_ in range(2):
            P.op("dve", lambda e, l_=l_: e.memset(wstb[64:128, l_, :, 0:64], 0.0), W=[("wstb",)])
        t3 = P.newtmp()
        t4 = P.newtmp()
        ld(tmp[0:1, t3, :], bsr_d[:, 0:512], [("tmp", t3)])
        ld(tmp[0:1, t4, :], bsr_d[:, 512:1024], [("tmp", t4)])
        P.op("dve", lambda e: e.tensor_copy(out=bsrb[0:1, 0:512], in_=tmp[0:1, t3, :]), R=[("tmp", t3)], W=[("bsrb",)])
        P.op("dve", lambda e: e.tensor_copy(out=bsrb[0:1, 512:1024], in_=tmp[0:1, t4, :]), R=[("tmp", t4)], W=[("bsrb",)])
        P.op("dve", lambda e: e.memset(onesb[:], 1.0), W=[("onesb",)])
        P.op("dve", lambda e: e.memset(epsc[:], EPS), W=[("epsc",)])
        P.op("dve", lambda e: e.memset(Tm[:], 0.0), W=[("Tm", 0), ("Tm", 1)])
        P.op("dve", lambda e: e.tensor_tensor(out=lb1[:], in0=hglb[:, 1, :], in1=hglb[:, 0, :], op=ALU.subtract),
             R=[("hglb",)], W=[("lb1",)])
        P.op("act", lambda e: e.activation(out=lb1[:], in_=lb1[:], func=AF.Sigmoid), R=[("lb1",)], W=[("lb1",)])
        P.op("dve", lambda e: e.tensor_scalar(out=oml[:], in0=lb1[:], scalar1=-1.0, scalar2=1.0, op0=ALU.mult, op1=ALU.add),
             R=[("lb1",)], W=[("oml",)])

        def group_src(l, g):
            if g < 14:
                return w_in[l].rearrange("(kc p) n -> p kc n", p=128)[:, :, g * 512:(g + 1) * 512], 8
            if g == G_UPA:
                return w_up_a[l].rearrange("(kc p) n -> p kc n", p=128), 4
            if g < G_OUT:
                j = g - G_UPB
                return w_up_b[l].rearrange("(kc p) n -> p kc n", p=128)[:, :, j * 512:(j + 1) * 512], 8
            if g < G_FF1:
                j = g - G_OUT
                return w_out[l].rearrange("(kc p) n -> p kc n", p=128)[:, :, j * 512:(j + 1) * 512], 8
            if g < G_FF2:
                j = g - G_FF1
                return w_ff1[l].rearrange("(kc p) n -> p kc n", p=128)[:, :, j * 512:(j + 1) * 512], 8
            j = g - G_FF2
            return w_ff2[l].rearrange("(kc p) n -> p kc n", p=128)[:, :, j * 128:(j + 1) * 128], 32

        def piece_src(l, g, q):
            r = lambda w: w[l].rearrange("(kc p) n -> p kc n", p=128)
            if g < 14:
                return r(w_in)[:, q, g * 512:(g + 1) * 512]
            if g == G_UPA:
                return r(w_up_a)[:, q // 2, (q % 2) * 512:(q % 2 + 1) * 512]
            if g < G_OUT:
                j = g - G_UPB
                return r(w_up_b)[:, q, j * 512:(j + 1) * 512]
            if g < G_FF1:
                j = g - G_OUT
                return r(w_out)[:, q, j * 512:(j + 1) * 512]
            if g < G_FF2:
                j = g - G_FF1
                return r(w_ff1)[:, q, j * 512:(j + 1) * 512]
            j = g - G_FF2
            return r(w_ff2)[:, 4 * q:4 * q + 4, j * 128:(j + 1) * 128]

        cv = {"n": 0, "pending": None}
        NCS = 6

        def layer_groups():
            return ([1, 0,
                     2, 3, 8, 9,
                     4, 6, 5, 7,
                     G_UPA, 10, 11,
                     12, G_UPB, 13, G_UPB + 1,
                     G_OUT, G_OUT + 1] +
                    list(range(G_FF1, G_FF1 + 8)) + list(range(G_FF2, G_FF2 + 8)))

        WQ = [(l, g) for _t in range(NT) for l in range(NL) for g in layer_groups()]
        wstate = {"issued": 0, "next": 0, "rel": {}, "cur": {}}
        wslot_of = {}

        def wkeys(slot):
            return [("w", slot, q) for q in range(8)]

        def flush_store():
            if cv["pending"] is not None:
                slot, l, g = cv["pending"]
                P.dma(lambda e, slot=slot, l=l, g=g: e.dma_start(out=wsc[l, g], in_=wsl[:, slot, :]), "cs%d" % slot,
                      R=wkeys(slot), W=[("wsc", l, g)])
                cv["pending"] = None

        def wissue():
            while wstate["issued"] < len(WQ):
                i = wstate["issued"]
                if i >= NWS and not wstate["rel"].get(i - NWS, False):
                    break
                l, g = WQ[i]
                slot = i % NWS
                if i < NL * NGRP:
                    for q in range(8):
                        st = cv["n"] % NCS
                        cv["n"] += 1
                        src = piece_src(l, g, q)
                        dst = cf32[:, st, :].rearrange("p (k c) -> p k c", k=4) if g >= G_FF2 else cf32[:, st, :]
                        P.dma(lambda e, dst=dst, src=src: e.dma_start(out=dst, in_=src), "cl%d" % st, W=[("cf", st)])
                        wdst = wsl[:, slot, q * 512:(q + 1) * 512]
                        if q % 2 == 0:
                            P.op("act", lambda e, st=st, wdst=wdst: e.activation(out=wdst, in_=cf32[:, st, :], func=AF.Copy),
                                 R=[("cf", st)], W=[("w", slot, q)])
                        else:
                            ceng = "dve" if q % 4 == 1 else "pool"
                            P.op(ceng, lambda e, st=st, wdst=wdst: e.tensor_copy(out=wdst, in_=cf32[:, st, :]),
                                 R=[("cf", st)], W=[("w", slot, q)])
                    flush_store()
                    cv["pending"] = (slot, l, g)
                else:
                    flush_store()
                    P.dma(lambda e, slot=slot, l=l, g=g: e.dma_start(out=wsl[:, slot, :], in_=wsc[l, g]), "w%d" % slot,
                          R=[("wsc", l, g)], W=wkeys(slot))
                wslot_of[i] = slot
                wstate["issued"] += 1

        def wget(l, g):
            i = wstate["next"]
            assert WQ[i] == (l, g), (WQ[i], l, g)
            wissue()
            assert i < wstate["issued"], "weight slot deadlock"
            wstate["next"] += 1
            wstate["cur"][(l, g)] = i
            return wslot_of[i]

        def wdone(l, g):
            i = wstate["cur"].pop((l, g))
            wstate["rel"][i] = True
            wissue()

        def proj_fm(slot, woff, nk, kstride, rhs, rkeys):
            bk = P.newbank()
            for kc in range(nk):
                P.op("pe", lambda e, kc=kc, bk=bk: e.matmul(
                    ps[:, bk, :], lhsT=wsl[:, slot, kc * kstride + woff: kc * kstride + woff + 128], rhs=rhs(kc),
                    start=(kc == 0), stop=(kc == nk - 1)), R=wkeys(slot) + [rkeys(kc)], W=[("ps", bk)])
            return bk

        hb_rhs = lambda kc: hb[:, kc, :]
        hb_key = lambda kc: ("hb", kc)

        st_ = {"bk": None}

        def stats_sq(c):
            s = c % 2
            P.op("act", lambda e, c=c, s=s: e.activation(out=sq[:, s, :], in_=x[:, c, :], func=AF.Square),
                 R=[("x", c)], W=[("sq", s)])

        def stats_mm(c):
            s = c % 2
            bk = st_["bk"]
            P.op("pe", lambda e, c=c, s=s, bk=bk: e.matmul(ps[:, bk, :], lhsT=onesb[:], rhs=sq[:, s, :],
                                                           start=(c == 0), stop=(c == 7)),
                 R=[("sq", s), ("onesb",)], W=[("ps", bk)])

        def stats_begin():
            st_["bk"] = P.newbank()
            P.reserved = st_["bk"]

        def rmsnorm_to_hb(gcol, have_stats=False):
            if not have_stats:
                stats_begin()
                for c in range(8):
                    stats_sq(c)
                    stats_mm(c)
            bk = st_["bk"]
            P.reserved = None
            P.op("act", lambda e: e.activation(out=msv[:], in_=ps[:, bk, :], func=AF.Sqrt, scale=1.0 / D, bias=epsc[:, 0:1]),
                 R=[("ps", bk), ("epsc",)], W=[("msv",)])
            P.op("dve", lambda e: e.reciprocal(out=rstd[:], in_=msv[:]), R=[("msv",)], W=[("rstd",)])

        def norm_apply(gcol, c, out_ap, wkeys, eng="dve"):
            P.op(eng, lambda e: e.scalar_tensor_tensor(out=out_ap, in0=x[:, c, :], scalar=gn[:, gcol + c:gcol + c + 1],
                                                       in1=rstd[:], op0=ALU.mult, op1=ALU.mult),
                 R=[("x", c), ("rstd",), ("gn",)], W=wkeys)

        for ti in range(NT):
            tsl = slice(ti * T, (ti + 1) * T)
            for c in range(8):
                P.dma(lambda e, tsl=tsl, c=c: e.dma_start(out=x[:, c, :], in_=xT[c * 128:(c + 1) * 128, tsl]),
                      "xld%d" % c, W=[("x", c)])
            for l in range(NL):
                P.mark("rms")
                rmsnorm_to_hb(l * 8, have_stats=(l > 0))
                for c in range(8):
                    norm_apply(l * 8, c, hb[:, c, :], [("hb", c)])

                P.mark("zv")
                slot = wget(l, 1)
                for b in range(NB):
                    bk = P.newbank()
                    for kc in range(8):
                        P.op("pe", lambda e, kc=kc, bk=bk, b=b, slot=slot: e.matmul(
                            ps[:, bk, :], lhsT=hb[:, kc, b * 128:(b + 1) * 128], rhs=wsl[:, slot, kc * 512:(kc + 1) * 512],
                            start=(kc == 0), stop=(kc == 7)), R=wkeys(slot) + [("hb", kc)], W=[("ps", bk)])
                    ta = P.newtmp()
                    tb = P.newtmp()
                    sp_ = b % 2
                    P.op("act", lambda e, bk=bk, ta=ta, sp_=sp_: e.activation(
                        out=tmp[:, ta, :], in_=ps[:, bk, :], func=AF.Gelu, accum_out=lnst[:, sp_, 0:1]),
                        R=[("ps", bk)], W=[("tmp", ta), ("lnst", sp_)])
                    P.op("act", lambda e, ta=ta, tb=tb, sp_=sp_: e.activation(
                        out=tmp[:, tb, :], in_=tmp[:, ta, :], func=AF.Square, accum_out=lnst[:, sp_, 1:2]),
                        R=[("tmp", ta), ("lnst", sp_)], W=[("tmp", tb), ("lnst", sp_)])
                    lk = [("lnst", sp_)]
                    P.op("dve", lambda e, sp_=sp_: e.tensor_scalar(
                        out=lnst[:, sp_, 2:3], in0=lnst[:, sp_, 0:1], scalar1=1.0 / 512, scalar2=None, op0=ALU.mult),
                        R=lk, W=lk)
                    P.op("dve", lambda e, sp_=sp_: e.tensor_tensor(
                        out=lnst[:, sp_, 3:4], in0=lnst[:, sp_, 2:3], in1=lnst[:, sp_, 2:3], op=ALU.mult), R=lk, W=lk)
                    P.op("dve", lambda e, sp_=sp_: e.scalar_tensor_tensor(
                        out=lnst[:, sp_, 4:5], in0=lnst[:, sp_, 1:2], scalar=1.0 / 512, in1=lnst[:, sp_, 3:4],
                        op0=ALU.mult, op1=ALU.subtract), R=lk, W=lk)
                    P.op("act", lambda e, sp_=sp_: e.activation(
                        out=lnst[:, sp_, 6:7], in_=lnst[:, sp_, 4:5], func=AF.Sqrt, bias=epsc[:, 0:1]), R=lk + [("epsc",)], W=lk)
                    P.op("dve", lambda e, sp_=sp_: e.reciprocal(out=lnst[:, sp_, 5:6], in_=lnst[:, sp_, 6:7]), R=lk, W=lk)
                    P.op("dve", lambda e, ta=ta, tb=tb, sp_=sp_: e.tensor_scalar(
                        out=tmp[:, tb, :], in0=tmp[:, ta, :], scalar1=lnst[:, sp_, 2:3], scalar2=lnst[:, sp_, 5:6],
                        op0=ALU.subtract, op1=ALU.mult), R=[("tmp", ta), ("lnst", sp_)], W=[("tmp", tb)])
                    P.op("pool", lambda e, tb=tb, l=l: e.tensor_tensor(
                        out=tmp[:, tb, :], in0=tmp[:, tb, :], in1=lng[:, l, :], op=ALU.mult),
                        R=[("tmp", tb), ("lng",)], W=[("tmp", tb)])
                    P.op("pool", lambda e, tb=tb, l=l, b=b: e.tensor_tensor(
                        out=v[:, b, :], in0=tmp[:, tb, :], in1=lnb[:, l, :], op=ALU.add),
                        R=[("tmp", tb), ("lnb",)], W=[("v", b)])
                wdone(l, 1)
                P.mark("zu")
                slot = wget(l, 0)
                for m in range(4):
                    bk = proj_fm(slot, m * 128, 8, 512, hb_rhs, hb_key)
                    P.op("act", lambda e, bk=bk, m=m: e.activation(out=u[:, m, :], in_=ps[:, bk, :], func=AF.Gelu),
                         R=[("ps", bk)], W=[("u", m)])
                wdone(l, 0)
                P.mark("zqzo")
                for base, g0 in ((0, 2), (24, 8)):
                    for hd in range(8):
                        if hd % 4 == 0:
                            if hd:
                                wdone(l, g0)
                            slot = wget(l, g0 + hd // 4)
                        bk = proj_fm(slot, (hd % 4) * 128, 8, 512, hb_rhs, hb_key)
                        P.op("act", lambda e, bk=bk, i=base + hd: e.activation(out=big[:, i, :], in_=ps[:, bk, :], func=AF.Silu),
                             R=[("ps", bk)], W=[("big", base + hd)])
                    wdone(l, g0 + 1)
                P.mark("spat")
                for g in range(4):
                    bk = P.newbank()
                    for b in range(NB):
                        osl = ps[:, bk, b * 128:(b + 1) * 128]
                        bi = (l * 4 + g) * 128
                        P.op("pe", lambda e, osl=osl, bi=bi: e.matmul(
                            osl, lhsT=onesb[0:1, :], rhs=bsrb[0:1, bi:bi + 128], start=True, stop=False),
                            R=[("onesb",), ("bsrb",)], W=[("ps", bk)])
                        P.op("pe", lambda e, osl=osl, b=b, g=g, l=l: e.matmul(
                            osl, lhsT=v[:, b, g * 128:(g + 1) * 128], rhs=wstb[:, l, g, :], start=False, stop=True),
                            R=[("v", b), ("wstb",)], W=[("ps", bk)])
                    P.op("dve", lambda e, bk=bk, g=g: e.tensor_tensor(out=u[:, g, :], in0=ps[:, bk, :], in1=u[:, g, :],
                                                                      op=ALU.mult),
                         R=[("ps", bk), ("u", g)], W=[("u", g)])
                ga_st = {}

                def gateA_chunk(c):
                    if c == 0:
                        ga_st["a"] = wget(l, G_UPA)
                        ga_st["g"] = wget(l, 10)
                    if c == 4:
                        wdone(l, 10)
                        ga_st["g"] = wget(l, 11)
                    slot_a, slot_g = ga_st["a"], ga_st["g"]
                    bkg = proj_fm(slot_g, (c % 4) * 128, 8, 512, hb_rhs, hb_key)
                    s_ = c % 2
                    P.op("act", lambda e, bkg=bkg, s_=s_: e.activation(out=sgr[:, s_, :], in_=ps[:, bkg, :], func=AF.Sigmoid),
                         R=[("ps", bkg)], W=[("sgr", s_)])
                    bky = proj_fm(slot_a, c * 128, 4, 1024, lambda kc: u[:, kc, :], lambda kc: ("u", kc))
                    P.op("dve", lambda e, bky=bky, s_=s_, c=c: e.tensor_tensor(
                        out=mixA[:, c, :], in0=ps[:, bky, :], in1=sgr[:, s_, :], op=ALU.mult),
                        R=[("ps", bky), ("sgr", s_)], W=[("mixA", c)])
                    if c == 7:
                        wdone(l, 11)
                        wdone(l, G_UPA)

                P.mark("zf")

                def vtok(b, hd, rows=slice(0, 128)):
                    return big[rows, 16 + b * 2 + hd // 4, (hd % 4) * 128:(hd % 4) * 128 + 128]

                vkey = lambda b, hd: ("big", 16 + b * 2 + hd // 4)
                sp_, sm_ = (-1.0, 1.0) if l == 0 else (1.0, -1.0)

                def zf_x(st):
                    hd, A, B, C, Dm = st
                    P.op("act", lambda e, B=B, C=C, sp_=sp_: e.activation(out=tmp[:, B, :], in_=tmp[:, C, :], func=AF.Exp, scale=sp_),
                         R=[("tmp", C)], W=[("tmp", B)])
                    P.op("act", lambda e, Dm=Dm, C=C, sm_=sm_: e.activation(out=tmp[:, Dm, :], in_=tmp[:, C, :], func=AF.Exp, scale=sm_),
                         R=[("tmp", C)], W=[("tmp", Dm)])

                def zf_kq(st):
                    hd, A, B, C, Dm = st
                    P.op("dve", lambda e, A=A, Dm=Dm, hd=hd: e.scalar_tensor_tensor(
                        out=big[:, 8 + hd, :], in0=tmp[:, A, :], scalar=1.0, in1=tmp[:, Dm, :], op0=ALU.subtract, op1=ALU.mult),
                        R=[("tmp", A), ("tmp", Dm)], W=[("big", 8 + hd)])
                    P.op("dve", lambda e, B=B, hd=hd: e.scalar_tensor_tensor(
                        out=big[:, hd, :], in0=big[:, hd, :], scalar=-1.0, in1=tmp[:, B, :], op0=ALU.mult, op1=ALU.mult),
                        R=[("big", hd), ("tmp", B)], W=[("big", hd)])
                    P.op("pool", lambda e, B=B, hd=hd, l=l, ti=ti: e.tensor_copy(
                        out=Et[:, l, hd, ti * 8:(ti + 1) * 8],
                        in_=tmp[:, B, :].rearrange("p (c t) -> p c t", t=64)[:, :, 63]),
                        R=[("tmp", B)], W=[("Et", l)])

                prev = None
                slot_f = slot_i = None
                for hd in range(8):
                    if hd % 4 == 0:
                        if hd:
                            wdone(l, 4)
                            wdone(l, 6)
                        slot_f = wget(l, 4 + hd // 4)
                        slot_i = wget(l, 6 + hd // 4)
                    bk = proj_fm(slot_f, (hd % 4) * 128, 8, 512, hb_rhs, hb_key)
                    A, B, C, Dm = P.newtmp(), P.newtmp(), P.newtmp(), P.newtmp()
                    P.op("act", lambda e, bk=bk, A=A: e.activation(out=tmp[:, A, :], in_=ps[:, bk, :], func=AF.Exp, scale=-1.0),
                         R=[("ps", bk)], W=[("tmp", A)])
                    P.op("act", lambda e, A=A, B=B: e.activation(out=tmp[:, B, :], in_=tmp[:, A, :], func=AF.Ln, bias=1.0),
                         R=[("tmp", A)], W=[("tmp", B)])
                    P.op("act", lambda e, A=A, B=B: e.activation(out=tmp[:, A, :], in_=tmp[:, B, :], func=AF.Exp, scale=-1.0),
                         R=[("tmp", B)], W=[("tmp", A)])
                    if l == 1:
                        P.op("dve", lambda e, A=A, hd=hd: e.tensor_scalar(
                            out=tmp[:, A, :], in0=tmp[:, A, :], scalar1=oml[:, hd:hd + 1], scalar2=lb1[:, hd:hd + 1],
                            op0=ALU.mult, op1=ALU.add), R=[("tmp", A), ("oml",), ("lb1",)], W=[("tmp", A)])
                        if prev is not None:
                            zf_x(prev)
                        P.op("act", lambda e, A=A, B=B: e.activation(out=tmp[:, B, :], in_=tmp[:, A, :], func=AF.Ln),
                             R=[("tmp", A)], W=[("tmp", B)])
                    elif prev is not None:
                        zf_x(prev)
                    P.op("dve", lambda e, B=B, C=C: e.tensor_tensor_scan(
                        out=tmp[:, C, :], data0=scanm[:], data1=tmp[:, B, :], initial=0.0, op0=ALU.mult, op1=ALU.add),
                        R=[("tmp", B), ("scanm",)], W=[("tmp", C)])
                    if prev is not None:
                        zf_kq(prev)
                    prev = (hd, A, B, C, Dm)
                    half, b = hd // 4, hd % 4
                    bk = P.newbank()
                    for kc in range(8):
                        P.op("pe", lambda e, kc=kc, bk=bk, b=b, slot=slot_i: e.matmul(
                            ps[:, bk, :], lhsT=hb[:, kc, b * 128:(b + 1) * 128], rhs=wsl[:, slot, kc * 512:(kc + 1) * 512],
                            start=(kc == 0), stop=(kc == 7)), R=wkeys(slot_i) + [("hb", kc)], W=[("ps", bk)])
                    i = 16 + b * 2 + half
                    if hd % 2 == 0:
                        P.op("dve", lambda e, bk=bk, i=i: e.tensor_copy(out=big[:, i, :], in_=ps[:, bk, :]),
                             R=[("ps", bk)], W=[("big", i)])
                    else:
                        P.op("act", lambda e, bk=bk, i=i: e.activation(out=big[:, i, :], in_=ps[:, bk, :], func=AF.Copy),
                             R=[("ps", bk)], W=[("big", i)])
                zf_x(prev)
                zf_kq(prev)
                wdone(l, 5)
                wdone(l, 7)

                P.mark("hgrn")
                hs = {}

                def st_S(b):
                    tok = slice(b * 128, (b + 1) * 128)
                    bS = [P.newbank(), P.newbank()]
                    for hd in range(8):
                        P.op("pe", lambda e, hd=hd, tok=tok, bS=bS: e.matmul(
                            ps[:, bS[hd // 4], (hd % 4) * 128:(hd % 4) * 128 + 128], lhsT=big[:, 8 + hd, tok], rhs=big[:, hd, tok],
                            start=True, stop=True), R=[("big", 8 + hd), ("big", hd)], W=[("ps", bS[hd // 4])])
                    bT = [P.newbank(), P.newbank()]
                    for hd in range(8):
                        P.op("pe", lambda e, hd=hd, tok=tok, bT=bT: e.matmul(
                            ps[:, bT[hd // 4], (hd % 4) * 128:(hd % 4) * 128 + 128], lhsT=big[:, 8 + hd, tok], rhs=identb[:],
                            start=True, stop=True), R=[("big", 8 + hd), ("identb",)], W=[("ps", bT[hd // 4])])
                    hs[b] = {"bS": bS, "bT": bT}

                def ev_K(b):
                    bT = hs[b]["bT"]
                    for j in range(2):
                        P.op("act", lambda e, j=j, bT=bT: e.activation(
                            out=ktok[:, j * 512:(j + 1) * 512], in_=ps[:, bT[j], :], func=AF.Copy),
                            R=[("ps", bT[j])], W=[("ktok", j)])

                def ev_P(b):
                    bS = hs[b]["bS"]
                    for j in range(2):
                        P.op("dve", lambda e, j=j, bS=bS, b=b: e.tensor_tensor(
                            out=PT[:, b % 2, j * 512:(j + 1) * 512], in0=ps[:, bS[j], :], in1=maskst[:], op=ALU.mult),
                            R=[("ps", bS[j]), ("maskst",)], W=[("PT", b % 2, j)])

                def st_D(b):
                    bD = [[P.newbank(), P.newbank()], [P.newbank(), P.newbank()]]
                    for ch in range(2):
                        rows = slice(ch * 64, ch * 64 + 64)
                        for hd in range(8):
                            P.op("pe", lambda e, hd=hd, ch=ch, rows=rows, b=b, bD=bD: e.matmul(
                                ps[:, bD[ch][hd // 4], (hd % 4) * 128:(hd % 4) * 128 + 128],
                                lhsT=ktok[rows, hd * 128:(hd + 1) * 128], rhs=vtok(b, hd, rows), start=True, stop=True),
                                R=[("ktok", hd // 4), vkey(b, hd)], W=[("ps", bD[ch][hd // 4])])
                    hs[b]["bD"] = bD

                def chain(b):
                    bD = hs[b]["bD"]
                    for ch in range(2):
                        gch = (ti * NB + b) * 2 + ch
                        if gch == 0:
                            P.op("dve", lambda e, ch=ch: e.memset(U[:, :, ch, :], 0.0), W=[("U", ch)])
                        else:
                            Ebc = Et[:, l, :, gch - 1:gch].to_broadcast([128, 8, 128])
                            P.op("dve", lambda e, ch=ch, l=l, Ebc=Ebc: e.tensor_tensor(
                                out=U[:, :, ch, :], in0=Tm[:, l, :, :], in1=Ebc, op=ALU.mult),
                                R=[("Tm", l), ("Et", l)], W=[("U", ch)])
                            P.op("dve", lambda e, l=l, Ebc=Ebc: e.tensor_tensor(
                                out=Tm[:, l, :, :], in0=Tm[:, l, :, :], in1=Ebc, op=ALU.mult),
                                R=[("Tm", l), ("Et", l)], W=[("Tm", l)])
                        for j in range(2):
                            P.op("dve", lambda e, ch=ch, l=l, j=j, bD=bD: e.tensor_tensor(
                                out=Tm[:, l, 4 * j:4 * j + 4, :], in0=Tm[:, l, 4 * j:4 * j + 4, :],
                                in1=ps[:, bD[ch][j], :].rearrange("p (h v) -> p h v", h=4), op=ALU.add),
                                R=[("Tm", l), ("ps", bD[ch][j])], W=[("Tm", l)])

                def st_O(b):
                    bO = [P.newbank(), P.newbank()]
                    for hd in range(8):
                        osl = lambda a, z, hd=hd, bO=bO: ps[:, bO[hd // 4], (hd % 4) * 128 + a:(hd % 4) * 128 + z]
                        P.op("pe", lambda e, hd=hd, b=b, osl=osl: e.matmul(
                            osl(0, 128), lhsT=vtok(b, hd), rhs=PT[:, b % 2, hd * 128:(hd + 1) * 128], start=True, stop=False),
                            R=[vkey(b, hd), ("PT", b % 2, hd // 4)], W=[("ps", bO[hd // 4])])
                        for ch in range(2):
                            P.op("pe", lambda e, hd=hd, b=b, ch=ch, osl=osl: e.matmul(
                                osl(ch * 64, ch * 64 + 64), lhsT=U[:, hd, ch, :],
                                rhs=big[:, hd, b * 128 + ch * 64:b * 128 + ch * 64 + 64], start=False, stop=(ch == 1)),
                                R=[("U", ch), ("big", hd)], W=[("ps", bO[hd // 4])])
                    hs[b]["bO"] = bO

                def ev_sq(b):
                    bO = hs[b]["bO"]
                    for j in range(2):
                        P.op("act", lambda e, j=j, bO=bO: e.activation(
                            out=osq[:, j * 512:(j + 1) * 512], in_=ps[:, bO[j], :], func=AF.Square),
                            R=[("ps", bO[j])], W=[("osq", j)])
                        P.op("act", lambda e, j=j, bO=bO, l=l: e.activation(
                            out=cf32[:, j, :], in_=ps[:, bO[j], :], func=AF.Copy, scale=hgg[:, l:l + 1]),
                            R=[("ps", bO[j]), ("hgg",)], W=[("cf", j)])

                def st_M(b):
                    tok = slice(b * 128, (b + 1) * 128)
                    bO = hs[b]["bO"]
                    bM = [P.newbank(), P.newbank()]
                    for j in range(2):
                        P.op("pe", lambda e, j=j, bM=bM: e.matmul(
                            ps[:, bM[j], :], lhsT=onesb[:], rhs=osq[:, j * 512:(j + 1) * 512], start=True, stop=True),
                            R=[("osq", j), ("onesb",)], W=[("ps", bM[j])])
                    for j in range(2):
                        P.op("act", lambda e, j=j, bM=bM: e.activation(
                            out=msvo[:, j, :], in_=ps[:, bM[j], :], func=AF.Ln, scale=1.0 / 128, bias=epsc[:, 0:1]),
                            R=[("ps", bM[j]), ("epsc",)], W=[("msvo", j)])
                        P.op("act", lambda e, j=j: e.activation(out=rso[:, j, :], in_=msvo[:, j, :], func=AF.Exp, scale=-0.5),
                             R=[("msvo", j)], W=[("rso", j)])
                        P.op("pool", lambda e, j=j: e.tensor_tensor(
                            out=otmp[:, j, :], in0=cf32[:, j, :], in1=rso[:, j, :], op=ALU.mult),
                            R=[("cf", j), ("rso", j)], W=[("otmp", j)])
                        P.op("pool", lambda e, j=j, tok=tok: e.tensor_tensor(
                            out=on[:, 4 * j:4 * j + 4, tok], in0=otmp[:, j, :].rearrange("p (h t) -> p h t", h=4),
                            in1=big[:, 24 + 4 * j:24 + 4 * j + 4, tok], op=ALU.mult),
                            R=[("otmp", j)] + [("big", 24 + 4 * j + i) for i in range(4)],
                            W=[("on", 4 * j + i) for i in range(4)])

                st_S(0)
                ev_K(0)
                ev_P(0)
                st_D(0)
                chain(0)
                for b in range(NB):
                    if b + 1 < NB:
                        st_S(b + 1)
                        ev_K(b + 1)
                        ev_P(b + 1)
                    st_O(b)
                    ev_sq(b)
                    if b + 1 < NB:
                        st_D(b + 1)
                        chain(b + 1)
                    gateA_chunk(2 * b)
                    st_M(b)
                    gateA_chunk(2 * b + 1)

                P.mark("gateB")
                slot_g = slot_b = None
                for c in range(8):
                    if c % 4 == 0:
                        if c:
                            wdone(l, 12)
                            wdone(l, G_UPB)
                        slot_g = wget(l, 12 + c // 4)
                        slot_b = wget(l, G_UPB + c // 4)
                    bkg = proj_fm(slot_g, (c % 4) * 128, 8, 512, hb_rhs, hb_key)
                    s_ = c % 2
                    P.op("act", lambda e, bkg=bkg, s_=s_: e.activation(out=sgr[:, s_, :], in_=ps[:, bkg, :], func=AF.Sigmoid),
                         R=[("ps", bkg)], W=[("sgr", s_)])
                    bky = proj_fm(slot_b, (c % 4) * 128, 8, 512, lambda kc: on[:, kc, :], lambda kc: ("on", kc))
                    P.op("dve", lambda e, bky=bky, s_=s_: e.tensor_tensor(
                        out=mt[:, s_, :], in0=ps[:, bky, :], in1=sgr[:, s_, :], op=ALU.mult),
                        R=[("ps", bky), ("sgr", s_)], W=[("mt", s_)])
                    P.op("pool", lambda e, s_=s_, c=c: e.tensor_tensor(
                        out=mixA[:, c, :], in0=mt[:, s_, :], in1=mixA[:, c, :], op=ALU.add),
                        R=[("mt", s_), ("mixA", c)], W=[("mixA", c)])
                wdone(l, 13)
                wdone(l, G_UPB + 1)
                P.mark("wout")
                for c in range(8):
                    if c % 4 == 0:
                        if c:
                            wdone(l, G_OUT)
                        slot = wget(l, G_OUT + c // 4)
                    if c == 0:
                        stats_begin()
                    bk = proj_fm(slot, (c % 4) * 128, 8, 512, lambda kc: mixA[:, kc, :], lambda kc: ("mixA", kc))
                    if c:
                        stats_mm(c - 1)
                    P.op("dve", lambda e, bk=bk, c=c: e.tensor_tensor(out=x[:, c, :], in0=ps[:, bk, :], in1=x[:, c, :], op=ALU.add),
                         R=[("ps", bk), ("x", c)], W=[("x", c)])
                    stats_sq(c)
                stats_mm(7)

                wdone(l, G_OUT + 1)
                P.mark("ffn1")
                rmsnorm_to_hb(16 + l * 8, have_stats=True)
                for c in range(8):
                    norm_apply(16 + l * 8, c, hb[:, c, :], [("hb", c)])
                for j in range(32):
                    if j % 4 == 0:
                        if j:
                            wdone(l, G_FF1 + j // 4 - 1)
                        slot = wget(l, G_FF1 + j // 4)
                    bk = proj_fm(slot, (j % 4) * 128, 8, 512, hb_rhs, hb_key)
                    tr = P.newtmp()
                    P.op("act", lambda e, bk=bk, tr=tr: e.activation(out=tmp[:, tr, :], in_=ps[:, bk, :], func=AF.Relu),
                         R=[("ps", bk)], W=[("tmp", tr)])
                    P.op("pool", lambda e, tr=tr, j=j: e.tensor_tensor(out=big[:, j, :], in0=tmp[:, tr, :], in1=tmp[:, tr, :],
                                                                        op=ALU.mult), R=[("tmp", tr)], W=[("big", j)])
                wdone(l, G_FF1 + 7)
                P.mark("ffn2")
                for c in range(8):
                    if c:
                        wdone(l, G_FF2 + c - 1)
                    slot = wget(l, G_FF2 + c)
                    if c == 0:
                        stats_begin()
                    bk = proj_fm(slot, 0, 32, 128, lambda kc: big[:, kc, :], lambda kc: ("big", kc))
                    if c:
                        stats_mm(c - 1)
                    P.op("dve", lambda e, bk=bk, c=c: e.tensor_tensor(out=x[:, c, :], in0=ps[:, bk, :], in1=x[:, c, :], op=ALU.add),
                         R=[("ps", bk), ("x", c)], W=[("x", c)])
                    stats_sq(c)
                stats_mm(7)
                wdone(l, G_FF2 + 7)

            P.mark("final")
            rmsnorm_to_hb(32, have_stats=True)
            for c in range(8):
                to = P.newtmp()
                norm_apply(32, c, tmp[:, to, :], [("tmp", to)])
                P.dma(lambda e, to=to, c=c, tsl=tsl: e.dma_start(out=yT[c * 128:(c + 1) * 128, tsl], in_=tmp[:, to, :]),
                      "o%d" % to, R=[("tmp", to)], W=[("yT", ti, c)])

        semnames = list(Prog.ENGS) + ["dma_" + s for s in P.dmacnt]
        sems = {n: es.enter_context(nc.semaphore(n)) for n in semnames}
        block = es.enter_context(nc.Block())
        P.emit(nc, block, sems)
    nc._marks = P.marks
    return nc


def _host_params(gmlp_ln_g, gmlp_ln_b, gmlp_ws, gmlp_bs, hg_lb, hg_norm_g, norm_mix, norm_ffn, final_norm):
    f = np.float32
    gn = np.zeros((128, 40), f)
    for l in range(2):
        gn[:, l * 8:(l + 1) * 8] = np.asarray(norm_mix[l], f).reshape(8, 128).T
        gn[:, 16 + l * 8:16 + (l + 1) * 8] = np.asarray(norm_ffn[l], f).reshape(8, 128).T
    gn[:, 32:40] = np.asarray(final_norm, f).reshape(8, 128).T
    lng = np.ascontiguousarray(np.broadcast_to(np.asarray(gmlp_ln_g, f)[None], (128, 2, 512)))
    lnb = np.ascontiguousarray(np.broadcast_to(np.asarray(gmlp_ln_b, f)[None], (128, 2, 512)))
    wst = np.ascontiguousarray(np.asarray(gmlp_ws, f).transpose(3, 0, 1, 2).reshape(128, 1024))
    bsr = np.ascontiguousarray(np.asarray(gmlp_bs, f).reshape(1, 1024))
    hglb = np.ascontiguousarray(np.asarray(hg_lb, f).reshape(2, 8, 128).transpose(2, 0, 1))
    hgg = np.ascontiguousarray(np.asarray(hg_norm_g, f).T)
    ident = np.eye(128, dtype=f)
    s = np.arange(128)[:, None]
    t = np.arange(128)[None, :]
    m = ((s // 64 == t // 64) & (s <= t)).astype(f)
    maskst = np.ascontiguousarray(np.tile(m, (1, 4)))
    scanm = np.ones((128, 512), f)
    scanm[:, ::64] = 0.0
    return dict(gn=gn, lng=lng, lnb=lnb, wst=wst, bsr=bsr, hglb=hglb, hgg=hgg, ident=ident, maskst=maskst, scanm=scanm)


_NC_CACHE = {}


def _run(x, w_in, gmlp_ln_g, gmlp_ln_b, gmlp_ws, gmlp_bs, w_up_a, hg_lb, hg_norm_g, w_up_b, w_out, norm_mix, norm_ffn,
         w_ff1, w_ff2, final_norm, NL=2, trace=False):
    x = np.asarray(x, np.float32)
    B, S, _ = x.shape
    NT = S // T
    key = (NT, NL)
    if key not in _NC_CACHE:
        _NC_CACHE[key] = build_nc(NT, NL)
    nc = _NC_CACHE[key]
    shared = _host_params(gmlp_ln_g, gmlp_ln_b, gmlp_ws, gmlp_bs, hg_lb, hg_norm_g, norm_mix, norm_ffn, final_norm)
    c = lambda a: np.ascontiguousarray(np.asarray(a, np.float32))
    shared.update(w_in=c(w_in), w_up_a=c(w_up_a), w_up_b=c(w_up_b), w_out=c(w_out), w_ff1=c(w_ff1), w_ff2=c(w_ff2))
    in_maps = []
    for b in range(B):
        m = dict(shared)
        m["xT"] = np.ascontiguousarray(x[b].T)
        in_maps.append(m)
    res = run_bass_kernel_spmd(nc, in_maps, core_ids=list(range(B)), **({"trace": True} if trace else {}))
    out = np.stack([np.ascontiguousarray(r["yT"].T) for r in res.results], axis=0)
    return out.astype(np.float32), res


def kernel(x, w_in, gmlp_ln_g, gmlp_ln_b, gmlp_ws, gmlp_bs, w_up_a, hg_lb, hg_norm_g, w_up_b, w_out, norm_mix, norm_ffn,
           w_ff1, w_ff2, final_norm):
    out, _ = _run(x, w_in, gmlp_ln_g, gmlp_ln_b, gmlp_ws, gmlp_bs, w_up_a, hg_lb, hg_norm_g, w_up_b, w_out, norm_mix,
                  norm_ffn, w_ff1, w_ff2, final_norm)
    return out
```

```python
from contextlib import ExitStack
import numpy as np
import concourse.bass as bass
import concourse.mybir as mybir
from concourse.bass_utils import run_bass_kernel_spmd

F32 = mybir.dt.float32
BF16 = mybir.dt.bfloat16
AF = mybir.ActivationFunctionType
ALU = mybir.AluOpType

D = 1024
T = 512
NB = 4
EPS = 1e-6
NGRP = 35
NWS = 4
NTMP = 8
G_UPA, G_UPB, G_OUT, G_FF1, G_FF2 = 14, 15, 17, 19, 27


class _Op:
    __slots__ = ("fn", "deps", "signal", "dma_sem", "cnt")


class Prog:
    ENGS = ("pe", "act", "dve", "pool", "sp")

    def __init__(self):
        self.ops = {n: [] for n in self.ENGS}
        self.bufs = {}
        self.dmacnt = {}
        self.bank = 0
        self.tmpi = 0
        self.marks = []
        self.reserved = None

    def mark(self, name):
        self.marks.append((name, len(self.ops["pe"])))

    def newbank(self):
        b = self.bank
        if b == self.reserved:
            b = (b + 1) % 8
        self.bank = (b + 1) % 8
        return b

    def newtmp(self):
        t = self.tmpi
        self.tmpi = (t + 1) % NTMP
        return t

    def _deps(self, R, W):
        deps = []
        for k in R:
            st = self.bufs.get(k)
            if st is not None and st[0] is not None:
                deps.append(st[0])
        for k in W:
            st = self.bufs.get(k)
            if st is not None:
                if st[0] is not None:
                    deps.append(st[0])
                for e, i in st[1].items():
                    deps.append(("e", e, i))
                deps.extend(st[2])
        return deps

    def _record(self, ev, R, W):
        for k in R:
            st = self.bufs.setdefault(k, [None, {}, []])
            if ev[0] == "e":
                st[1][ev[1]] = ev[2]
            else:
                st[2].append(ev)
        for k in W:
            self.bufs[k] = [ev, {}, []]

    def _mk(self, eng, fn, R, W, dma_sem=None):
        o = _Op()
        o.fn = fn
        o.signal = False
        o.dma_sem = dma_sem
        o.cnt = 0
        deps = []
        for d in self._deps(R, W):
            if d[0] == "e":
                if d[1] == eng and eng == "pe":
                    continue
                self.ops[d[1]][d[2]].signal = True
                deps.append(d)
            else:
                deps.append(("d", d[1], self.dmacnt[d[1]]))
        o.deps = deps
        self.ops[eng].append(o)
        return len(self.ops[eng]) - 1

    def op(self, eng, fn, R=(), W=()):
        idx = self._mk(eng, fn, R, W)
        self._record(("e", eng, idx), R, W)

    def dma(self, fn, sem, R=(), W=()):
        self._mk("sp", fn, R, W, dma_sem=sem)
        self.dmacnt[sem] = self.dmacnt.get(sem, 0) + 16
        self._record(("d", sem, self.dmacnt[sem]), R, W)

    def emit(self, nc, block, sems):
        for n in self.ENGS:
            c = 0
            for o in self.ops[n]:
                if o.signal and o.dma_sem is None:
                    c += 1
                o.cnt = c
        ops = self.ops

        def run(engname):
            def body(e):
                waited = {}
                for o in ops[engname]:
                    for d in o.deps:
                        if d[0] == "e":
                            key = d[1]
                            val = ops[d[1]][d[2]].cnt
                        else:
                            key = "dma_" + d[1]
                            val = d[2]
                        if waited.get(key, 0) < val:
                            e.wait_ge(sems[key], val)
                            waited[key] = val
                    ins = o.fn(e)
                    if o.dma_sem is not None:
                        ins.then_inc(sems["dma_" + o.dma_sem], 16)
                    elif o.signal:
                        ins.then_inc(sems[engname], 1)
                if engname == "sp":
                    for s, v in self.dmacnt.items():
                        if waited.get("dma_" + s, 0) < v:
                            e.wait_ge(sems["dma_" + s], v)
            return body

        block.sync(run("sp"))
        block.tensor(run("pe"))
        block.scalar(run("act"))
        block.vector(run("dve"))
        block.gpsimd(run("pool"))


def build_nc(NT, NL=2):
    S = NT * T
    NCH = NT * 8
    nc = bass.Bass("TRN2", target_bir_lowering=False)
    dt = lambda name, shape, dtype=F32, kind="ExternalInput": nc.dram_tensor(name, list(shape), dtype, kind=kind).ap()
    xT = dt("xT", [D, S])
    w_in = dt("w_in", [2, D, 7168])
    w_up_a = dt("w_up_a", [2, 512, D])
    w_up_b = dt("w_up_b", [2, D, D])
    w_out = dt("w_out", [2, D, D])
    w_ff1 = dt("w_ff1", [2, D, 4096])
    w_ff2 = dt("w_ff2", [2, 4096, D])
    gn_d = dt("gn", [128, 40])
    lng_d = dt("lng", [128, 2, 512])
    lnb_d = dt("lnb", [128, 2, 512])
    wst_d = dt("wst", [128, 1024])
    bsr_d = dt("bsr", [1, 1024])
    hglb_d = dt("hglb", [128, 2, 8])
    hgg_d = dt("hgg", [128, 2])
    ident_d = dt("ident", [128, 128])
    maskst_d = dt("maskst", [128, 512])
    scanm_d = dt("scanm", [128, 512])
    yT = dt("yT", [D, S], kind="ExternalOutput")
    wsc = dt("wsc", [2, NGRP, 128, 4096], BF16, kind="Internal")

    P = Prog()
    with ExitStack() as es:
        sb = lambda name, shape, dtype=F32: es.enter_context(nc.sbuf_tensor(name, list(shape), dtype))
        x = sb("x", [128, 8, T])
        hb = sb("hb", [128, 8, T], BF16)
        sq = sb("sq", [128, 2, T], BF16)
        rstd = sb("rstd", [128, T])
        msv = sb("msv", [128, T])
        big = sb("big", [128, 32, T], BF16)
        u = sb("u", [128, 4, T], BF16)
        v = sb("v", [128, 4, T], BF16)
        tmp = sb("tmp", [128, NTMP, T])
        sgr = sb("sgr", [128, 2, T])
        mt = sb("mt", [128, 2, T])
        mixA = sb("mixA", [128, 8, T], BF16)
        ktok = sb("ktok", [128, 1024], BF16)
        PT = sb("PT", [128, 2, 1024], BF16)
        osq = sb("osq", [128, 1024], BF16)
        U = sb("U", [128, 8, 2, 128], BF16)
        msvo = sb("msvo", [128, 2, T])
        rso = sb("rso", [128, 2, T])
        otmp = sb("otmp", [128, 2, T])
        on = sb("on", [128, 8, T], BF16)
        Tm = sb("Tm", [128, 2, 8, 128])
        Et = sb("Et", [128, 2, 8, NCH])
        wsl = sb("wsl", [128, NWS, 4096], BF16)
        identb = sb("identb", [128, 128], BF16)
        onesb = sb("onesb", [128, 128], BF16)
        maskst = sb("maskst_s", [128, 512])
        scanm = sb("scanm_s", [128, 512])
        lng = sb("lng_s", [128, 2, 512])
        lnb = sb("lnb_s", [128, 2, 512])
        wstb = sb("wstb", [128, 2, 4, 128], BF16)
        bsrb = sb("bsrb", [1, 1024], BF16)
        gn = sb("gn_s", [128, 40])
        hgg = sb("hgg_s", [128, 2])
        hglb = sb("hglb_s", [128, 2, 8])
        lb1 = sb("lb1", [128, 8])
        oml = sb("oml", [128, 8])
        lnst = sb("lnst", [128, 2, 8])
        epsc = sb("epsc", [128, 1])
        cf32 = sb("cf32", [128, 6, 512])
        ps = es.enter_context(nc.psum_tensor("ps", [128, 8, 512], F32))

        npar = [0]

        def ld(dst_ap, src_ap, W, R=()):
            P.dma(lambda e: e.dma_start(out=dst_ap, in_=src_ap), "par%d" % npar[0], R=R, W=W)
            npar[0] += 1

        ld(maskst[:], maskst_d[:, :], [("maskst",)])
        ld(scanm[:], scanm_d[:, :], [("scanm",)])
        ld(gn[:], gn_d[:, :], [("gn",)])
        ld(hgg[:], hgg_d[:, :], [("hgg",)])
        ld(hglb[:], hglb_d[:, :, :], [("hglb",)])
        ld(lng[:], lng_d[:, :, :], [("lng",)])
        ld(lnb[:], lnb_d[:, :, :], [("lnb",)])
        t0 = P.newtmp()
        ld(tmp[:, t0, 0:128], ident_d[:, :], [("tmp", t0)])
        P.op("dve", lambda e: e.tensor_copy(out=identb[:], in_=tmp[:, t0, 0:128]), R=[("tmp", t0)], W=[("identb",)])
        t1 = P.newtmp()
        t2 = P.newtmp()
        ld(tmp[:, t1, :], wst_d[:, 0:512], [("tmp", t1)])
        ld(tmp[:, t2, :], wst_d[:, 512:1024], [("tmp", t2)])
        P.op("dve", lambda e: e.tensor_copy(out=wstb[:, 0, :, :], in_=tmp[:, t1, :].rearrange("p (g q) -> p g q", g=4)),
             R=[("tmp", t1)], W=[("wstb",)])
        P.op("dve", lambda e: e.tensor_copy(out=wstb[:, 1, :, :], in_=tmp[:, t2, :].rearrange("p (g q) -> p g q", g=4)),
             R=[("tmp", t2)], W=[("wstb",)])
        for l_ in range(2):
            P.op("dve", lambda e, l_=l_: e.memset(wstb[64:128, l_, :, 0:64], 0.0), W=[("wstb",)])
        t3 = P.newtmp()
        t4 = P.newtmp()
        ld(tmp[0:1, t3, :], bsr_d[:, 0:512], [("tmp", t3)])
        ld(tmp[0:1, t4, :], bsr_d[:, 512:1024], [("tmp", t4)])
        P.op("dve", lambda e: e.tensor_copy(out=bsrb[0:1, 0:512], in_=tmp[0:1, t3, :]), R=[("tmp", t3)], W=[("bsrb",)])
        P.op("dve", lambda e: e.tensor_copy(out=bsrb[0:1, 512:1024], in_=tmp[0:1, t4, :]), R=[("tmp", t4)], W=[("bsrb",)])
        P.op("dve", lambda e: e.memset(onesb[:], 1.0), W=[("onesb",)])
        P.op("dve", lambda e: e.memset(epsc[:], EPS), W=[("epsc",)])
        P.op("dve", lambda e: e.memset(Tm[:], 0.0), W=[("Tm", 0), ("Tm", 1)])
        P.op("dve", lambda e: e.tensor_tensor(out=lb1[:], in0=hglb[:, 1, :], in1=hglb[:, 0, :], op=ALU.subtract),
             R=[("hglb",)], W=[("lb1",)])
        P.op("act", lambda e: e.activation(out=lb1[:], in_=lb1[:], func=AF.Sigmoid), R=[("lb1",)], W=[("lb1",)])
        P.op("dve", lambda e: e.tensor_scalar(out=oml[:], in0=lb1[:], scalar1=-1.0, scalar2=1.0, op0=ALU.mult, op1=ALU.add),
             R=[("lb1",)], W=[("oml",)])

        def group_src(l, g):
            if g < 14:
                return w_in[l].rearrange("(kc p) n -> p kc n", p=128)[:, :, g * 512:(g + 1) * 512], 8
            if g == G_UPA:
                return w_up_a[l].rearrange("(kc p) n -> p kc n", p=128), 4
            if g < G_OUT:
                j = g - G_UPB
                return w_up_b[l].rearrange("(kc p) n -> p kc n", p=128)[:, :, j * 512:(j + 1) * 512], 8
            if g < G_FF1:
                j = g - G_OUT
                return w_out[l].rearrange("(kc p) n -> p kc n", p=128)[:, :, j * 512:(j + 1) * 512], 8
            if g < G_FF2:
                j = g - G_FF1
                return w_ff1[l].rearrange("(kc p) n -> p kc n", p=128)[:, :, j * 512:(j + 1) * 512], 8
            j = g - G_FF2
            return w_ff2[l].rearrange("(kc p) n -> p kc n", p=128)[:, :, j * 128:(j + 1) * 128], 32

        def piece_src(l, g, q):
            r = lambda w: w[l].rearrange("(kc p) n -> p kc n", p=128)
            if g < 14:
                return r(w_in)[:, q, g * 512:(g + 1) * 512]
            if g == G_UPA:
                return r(w_up_a)[:, q // 2, (q % 2) * 512:(q % 2 + 1) * 512]
            if g < G_OUT:
                j = g - G_UPB
                return r(w_up_b)[:, q, j * 512:(j + 1) * 512]
            if g < G_FF1:
                j = g - G_OUT
                return r(w_out)[:, q, j * 512:(j + 1) * 512]
            if g < G_FF2:
                j = g - G_FF1
                return r(w_ff1)[:, q, j * 512:(j + 1) * 512]
            j = g - G_FF2
            return r(w_ff2)[:, 4 * q:4 * q + 4, j * 128:(j + 1) * 128]

        cv = {"n": 0, "pending": None}
        NCS = 6

        def layer_groups():
            return ([1, 0,
                     2, 3, 8, 9,
                     4, 6, 5, 7,
                     G_UPA, 10, 11,
                     12, G_UPB, 13, G_UPB + 1,
                     G_OUT, G_OUT + 1] +
                    list(range(G_FF1, G_FF1 + 8)) + list(range(G_FF2, G_FF2 + 8)))

        WQ = [(l, g) for _t in range(NT) for l in range(NL) for g in layer_groups()]
        wstate = {"issued": 0, "next": 0, "rel": {}, "cur": {}}
        wslot_of = {}

        def wkeys(slot):
            return [("w", slot, q) for q in range(8)]

        def flush_store():
            if cv["pending"] is not None:
                slot, l, g = cv["pending"]
                P.dma(lambda e, slot=slot, l=l, g=g: e.dma_start(out=wsc[l, g], in_=wsl[:, slot, :]), "cs%d" % slot,
                      R=wkeys(slot), W=[("wsc", l, g)])
                cv["pending"] = None

        def wissue():
            while wstate["issued"] < len(WQ):
                i = wstate["issued"]
                if i >= NWS and not wstate["rel"].get(i - NWS, False):
                    break
                l, g = WQ[i]
                slot = i % NWS
                if i < NL * NGRP:
                    for q in range(8):
                        st = cv["n"] % NCS
                        cv["n"] += 1
                        src = piece_src(l, g, q)
                        dst = cf32[:, st, :].rearrange("p (k c) -> p k c", k=4) if g >= G_FF2 else cf32[:, st, :]
                        P.dma(lambda e, dst=dst, src=src: e.dma_start(out=dst, in_=src), "cl%d" % st, W=[("cf", st)])
                        wdst = wsl[:, slot, q * 512:(q + 1) * 512]
                        if q % 2 == 0:
                            P.op("act", lambda e, st=st, wdst=wdst: e.activation(out=wdst, in_=cf32[:, st, :], func=AF.Copy),
                                 R=[("cf", st)], W=[("w", slot, q)])
                        else:
                            ceng = "dve" if q % 4 == 1 else "pool"
                            P.op(ceng, lambda e, st=st, wdst=wdst: e.tensor_copy(out=wdst, in_=cf32[:, st, :]),
                                 R=[("cf", st)], W=[("w", slot, q)])
                    flush_store()
                    cv["pending"] = (slot, l, g)
                else:
                    flush_store()
                    P.dma(lambda e, slot=slot, l=l, g=g: e.dma_start(out=wsl[:, slot, :], in_=wsc[l, g]), "w%d" % slot,
                          R=[("wsc", l, g)], W=wkeys(slot))
                wslot_of[i] = slot
                wstate["issued"] += 1

        def wget(l, g):
            i = wstate["next"]
            assert WQ[i] == (l, g), (WQ[i], l, g)
            wissue()
            assert i < wstate["issued"], "weight slot deadlock"
            wstate["next"] += 1
            wstate["cur"][(l, g)] = i
            return wslot_of[i]

        def wdone(l, g):
            i = wstate["cur"].pop((l, g))
            wstate["rel"][i] = True
            wissue()

        def proj_fm(slot, woff, nk, kstride, rhs, rkeys):
            bk = P.newbank()
            for kc in range(nk):
                P.op("pe", lambda e, kc=kc, bk=bk: e.matmul(
                    ps[:, bk, :], lhsT=wsl[:, slot, kc * kstride + woff: kc * kstride + woff + 128], rhs=rhs(kc),
                    start=(kc == 0), stop=(kc == nk - 1)), R=wkeys(slot) + [rkeys(kc)], W=[("ps", bk)])
            return bk

        hb_rhs = lambda kc: hb[:, kc, :]
        hb_key = lambda kc: ("hb", kc)

        st_ = {"bk": None}

        def stats_sq(c):
            s = c % 2
            P.op("act", lambda e, c=c, s=s: e.activation(out=sq[:, s, :], in_=x[:, c, :], func=AF.Square),
                 R=[("x", c)], W=[("sq", s)])

        def stats_mm(c):
            s = c % 2
            bk = st_["bk"]
            P.op("pe", lambda e, c=c, s=s, bk=bk: e.matmul(ps[:, bk, :], lhsT=onesb[:], rhs=sq[:, s, :],
                                                           start=(c == 0), stop=(c == 7)),
                 R=[("sq", s), ("onesb",)], W=[("ps", bk)])

        def stats_begin():
            st_["bk"] = P.newbank()
            P.reserved = st_["bk"]

        def rmsnorm_to_hb(gcol, have_stats=False):
            if not have_stats:
                stats_begin()
                for c in range(8):
                    stats_sq(c)
                    stats_mm(c)
            bk = st_["bk"]
            P.reserved = None
            P.op("act", lambda e: e.activation(out=msv[:], in_=ps[:, bk, :], func=AF.Sqrt, scale=1.0 / D, bias=epsc[:, 0:1]),
                 R=[("ps", bk), ("epsc",)], W=[("msv",)])
            P.op("dve", lambda e: e.reciprocal(out=rstd[:], in_=msv[:]), R=[("msv",)], W=[("rstd",)])

        def norm_apply(gcol, c, out_ap, wkeys, eng="dve"):
            P.op(eng, lambda e: e.scalar_tensor_tensor(out=out_ap, in0=x[:, c, :], scalar=gn[:, gcol + c:gcol + c + 1],
                                                       in1=rstd[:], op0=ALU.mult, op1=ALU.mult),
                 R=[("x", c), ("rstd",), ("gn",)], W=wkeys)

        for ti in range(NT):
            tsl = slice(ti * T, (ti + 1) * T)
            P.dma(lambda e, tsl=tsl: e.dma_start(out=x[:], in_=xT.rearrange("(c p) t -> p c t", p=128)[:, :, tsl]),
                  "xld", W=[("x", c) for c in range(8)])
            for l in range(NL):
                P.mark("rms")
                rmsnorm_to_hb(l * 8, have_stats=(l > 0))
                for c in range(8):
                    norm_apply(l * 8, c, hb[:, c, :], [("hb", c)])

                P.mark("zv")
                slot = wget(l, 1)
                for b in range(NB):
                    bk = P.newbank()
                    for kc in range(8):
                        P.op("pe", lambda e, kc=kc, bk=bk, b=b, slot=slot: e.matmul(
                            ps[:, bk, :], lhsT=hb[:, kc, b * 128:(b + 1) * 128], rhs=wsl[:, slot, kc * 512:(kc + 1) * 512],
                            start=(kc == 0), stop=(kc == 7)), R=wkeys(slot) + [("hb", kc)], W=[("ps", bk)])
                    ta = P.newtmp()
                    tb = P.newtmp()
                    sp_ = b % 2
                    P.op("act", lambda e, bk=bk, ta=ta, sp_=sp_: e.activation(
                        out=tmp[:, ta, :], in_=ps[:, bk, :], func=AF.Gelu, accum_out=lnst[:, sp_, 0:1]),
                        R=[("ps", bk)], W=[("tmp", ta), ("lnst", sp_)])
                    P.op("act", lambda e, ta=ta, tb=tb, sp_=sp_: e.activation(
                        out=tmp[:, tb, :], in_=tmp[:, ta, :], func=AF.Square, accum_out=lnst[:, sp_, 1:2]),
                        R=[("tmp", ta), ("lnst", sp_)], W=[("tmp", tb), ("lnst", sp_)])
                    lk = [("lnst", sp_)]
                    P.op("dve", lambda e, sp_=sp_: e.tensor_scalar(
                        out=lnst[:, sp_, 2:3], in0=lnst[:, sp_, 0:1], scalar1=1.0 / 512, scalar2=None, op0=ALU.mult),
                        R=lk, W=lk)
                    P.op("dve", lambda e, sp_=sp_: e.tensor_tensor(
                        out=lnst[:, sp_, 3:4], in0=lnst[:, sp_, 2:3], in1=lnst[:, sp_, 2:3], op=ALU.mult), R=lk, W=lk)
                    P.op("dve", lambda e, sp_=sp_: e.scalar_tensor_tensor(
                        out=lnst[:, sp_, 4:5], in0=lnst[:, sp_, 1:2], scalar=1.0 / 512, in1=lnst[:, sp_, 3:4],
                        op0=ALU.mult, op1=ALU.subtract), R=lk, W=lk)
                    P.op("act", lambda e, sp_=sp_: e.activation(
                        out=lnst[:, sp_, 6:7], in_=lnst[:, sp_, 4:5], func=AF.Sqrt, bias=epsc[:, 0:1]), R=lk + [("epsc",)], W=lk)
                    P.op("dve", lambda e, sp_=sp_: e.reciprocal(out=lnst[:, sp_, 5:6], in_=lnst[:, sp_, 6:7]), R=lk, W=lk)
                    P.op("dve", lambda e, ta=ta, tb=tb, sp_=sp_: e.tensor_scalar(
                        out=tmp[:, tb, :], in0=tmp[:, ta, :], scalar1=lnst[:, sp_, 2:3], scalar2=lnst[:, sp_, 5:6],
                        op0=ALU.subtract, op1=ALU.mult), R=[("tmp", ta), ("lnst", sp_)], W=[("tmp", tb)])
                    P.op("pool", lambda e, tb=tb, l=l: e.tensor_tensor(
                        out=tmp[:, tb, :], in0=tmp[:, tb, :], in1=lng[:, l, :], op=ALU.mult),
                        R=[("tmp", tb), ("lng",)], W=[("tmp", tb)])
                    P.op("pool", lambda e, tb=tb, l=l, b=b: e.tensor_tensor(
                        out=v[:, b, :], in0=tmp[:, tb, :], in1=lnb[:, l, :], op=ALU.add),
                        R=[("tmp", tb), ("lnb",)], W=[("v", b)])
                wdone(l, 1)
                P.mark("zu")
                slot = wget(l, 0)
                for m in range(4):
                    bk = proj_fm(slot, m * 128, 8, 512, hb_rhs, hb_key)
                    P.op("act", lambda e, bk=bk, m=m: e.activation(out=u[:, m, :], in_=ps[:, bk, :], func=AF.Gelu),
                         R=[("ps", bk)], W=[("u", m)])
                wdone(l, 0)
                P.mark("zqzo")
                for base, g0 in ((0, 2), (24, 8)):
                    for hd in range(8):
                        if hd % 4 == 0:
                            if hd:
                                wdone(l, g0)
                            slot = wget(l, g0 + hd // 4)
                        bk = proj_fm(slot, (hd % 4) * 128, 8, 512, hb_rhs, hb_key)
                        P.op("act", lambda e, bk=bk, i=base + hd: e.activation(out=big[:, i, :], in_=ps[:, bk, :], func=AF.Silu),
                             R=[("ps", bk)], W=[("big", base + hd)])
                    wdone(l, g0 + 1)
                P.mark("spat")
                for g in range(4):
                    bk = P.newbank()
                    for b in range(NB):
                        osl = ps[:, bk, b * 128:(b + 1) * 128]
                        bi = (l * 4 + g) * 128
                        P.op("pe", lambda e, osl=osl, bi=bi: e.matmul(
                            osl, lhsT=onesb[0:1, :], rhs=bsrb[0:1, bi:bi + 128], start=True, stop=False),
                            R=[("onesb",), ("bsrb",)], W=[("ps", bk)])
                        P.op("pe", lambda e, osl=osl, b=b, g=g, l=l: e.matmul(
                            osl, lhsT=v[:, b, g * 128:(g + 1) * 128], rhs=wstb[:, l, g, :], start=False, stop=True),
                            R=[("v", b), ("wstb",)], W=[("ps", bk)])
                    P.op("dve", lambda e, bk=bk, g=g: e.tensor_tensor(out=u[:, g, :], in0=ps[:, bk, :], in1=u[:, g, :],
                                                                      op=ALU.mult),
                         R=[("ps", bk), ("u", g)], W=[("u", g)])
                ga_st = {}

                def gateA_chunk(c):
                    if c == 0:
                        ga_st["a"] = wget(l, G_UPA)
                        ga_st["g"] = wget(l, 10)
                    if c == 4:
                        wdone(l, 10)
                        ga_st["g"] = wget(l, 11)
                    slot_a, slot_g = ga_st["a"], ga_st["g"]
                    bkg = proj_fm(slot_g, (c % 4) * 128, 8, 512, hb_rhs, hb_key)
                    s_ = c % 2
                    P.op("act", lambda e, bkg=bkg, s_=s_: e.activation(out=sgr[:, s_, :], in_=ps[:, bkg, :], func=AF.Sigmoid),
                         R=[("ps", bkg)], W=[("sgr", s_)])
                    bky = proj_fm(slot_a, c * 128, 4, 1024, lambda kc: u[:, kc, :], lambda kc: ("u", kc))
                    P.op("dve", lambda e, bky=bky, s_=s_, c=c: e.tensor_tensor(
                        out=mixA[:, c, :], in0=ps[:, bky, :], in1=sgr[:, s_, :], op=ALU.mult),
                        R=[("ps", bky), ("sgr", s_)], W=[("mixA", c)])
                    if c == 7:
                        wdone(l, 11)
                        wdone(l, G_UPA)

                P.mark("zf")

                def vtok(b, hd, rows=slice(0, 128)):
                    return big[rows, 16 + b * 2 + hd // 4, (hd % 4) * 128:(hd % 4) * 128 + 128]

                vkey = lambda b, hd: ("big", 16 + b * 2 + hd // 4)
                sp_, sm_ = (-1.0, 1.0) if l == 0 else (1.0, -1.0)

                def zf_x(st):
                    hd, A, B, C, Dm = st
                    P.op("act", lambda e, B=B, C=C, sp_=sp_: e.activation(out=tmp[:, B, :], in_=tmp[:, C, :], func=AF.Exp, scale=sp_),
                         R=[("tmp", C)], W=[("tmp", B)])
                    P.op("act", lambda e, Dm=Dm, C=C, sm_=sm_: e.activation(out=tmp[:, Dm, :], in_=tmp[:, C, :], func=AF.Exp, scale=sm_),
                         R=[("tmp", C)], W=[("tmp", Dm)])

                def zf_kq(st):
                    hd, A, B, C, Dm = st
                    P.op("dve", lambda e, A=A, Dm=Dm, hd=hd: e.scalar_tensor_tensor(
                        out=big[:, 8 + hd, :], in0=tmp[:, A, :], scalar=1.0, in1=tmp[:, Dm, :], op0=ALU.subtract, op1=ALU.mult),
                        R=[("tmp", A), ("tmp", Dm)], W=[("big", 8 + hd)])
                    P.op("dve", lambda e, B=B, hd=hd: e.scalar_tensor_tensor(
                        out=big[:, hd, :], in0=big[:, hd, :], scalar=-1.0, in1=tmp[:, B, :], op0=ALU.mult, op1=ALU.mult),
                        R=[("big", hd), ("tmp", B)], W=[("big", hd)])
                    P.op("pool", lambda e, B=B, hd=hd, l=l, ti=ti: e.tensor_copy(
                        out=Et[:, l, hd, ti * 8:(ti + 1) * 8],
                        in_=tmp[:, B, :].rearrange("p (c t) -> p c t", t=64)[:, :, 63]),
                        R=[("tmp", B)], W=[("Et", l)])

                prev = None
                slot_f = slot_i = None
                for hd in range(8):
                    if hd % 4 == 0:
                        if hd:
                            wdone(l, 4)
                            wdone(l, 6)
                        slot_f = wget(l, 4 + hd // 4)
                        slot_i = wget(l, 6 + hd // 4)
                    bk = proj_fm(slot_f, (hd % 4) * 128, 8, 512, hb_rhs, hb_key)
                    A, B, C, Dm = P.newtmp(), P.newtmp(), P.newtmp(), P.newtmp()
                    P.op("act", lambda e, bk=bk, A=A: e.activation(out=tmp[:, A, :], in_=ps[:, bk, :], func=AF.Exp, scale=-1.0),
                         R=[("ps", bk)], W=[("tmp", A)])
                    P.op("act", lambda e, A=A, B=B: e.activation(out=tmp[:, B, :], in_=tmp[:, A, :], func=AF.Ln, bias=1.0),
                         R=[("tmp", A)], W=[("tmp", B)])
                    P.op("act", lambda e, A=A, B=B: e.activation(out=tmp[:, A, :], in_=tmp[:, B, :], func=AF.Exp, scale=-1.0),
                         R=[("tmp", B)], W=[("tmp", A)])
                    if l == 1:
                        P.op("dve", lambda e, A=A, hd=hd: e.tensor_scalar(
                            out=tmp[:, A, :], in0=tmp[:, A, :], scalar1=oml[:, hd:hd + 1], scalar2=lb1[:, hd:hd + 1],
                            op0=ALU.mult, op1=ALU.add), R=[("tmp", A), ("oml",), ("lb1",)], W=[("tmp", A)])
                        if prev is not None:
                            zf_x(prev)
                        P.op("act", lambda e, A=A, B=B: e.activation(out=tmp[:, B, :], in_=tmp[:, A, :], func=AF.Ln),
                             R=[("tmp", A)], W=[("tmp", B)])
                    elif prev is not None:
                        zf_x(prev)
                    P.op("dve", lambda e, B=B, C=C: e.tensor_tensor_scan(
                        out=tmp[:, C, :], data0=scanm[:], data1=tmp[:, B, :], initial=0.0, op0=ALU.mult, op1=ALU.add),
                        R=[("tmp", B), ("scanm",)], W=[("tmp", C)])
                    if prev is not None:
                        zf_kq(prev)
                    prev = (hd, A, B, C, Dm)
                    half, b = hd // 4, hd % 4
                    bk = P.newbank()
                    for kc in range(8):
                        P.op("pe", lambda e, kc=kc, bk=bk, b=b, slot=slot_i: e.matmul(
                            ps[:, bk, :], lhsT=hb[:, kc, b * 128:(b + 1) * 128], rhs=wsl[:, slot, kc * 512:(kc + 1) * 512],
                            start=(kc == 0), stop=(kc == 7)), R=wkeys(slot_i) + [("hb", kc)], W=[("ps", bk)])
                    i = 16 + b * 2 + half
                    if hd % 2 == 0:
                        P.op("dve", lambda e, bk=bk, i=i: e.tensor_copy(out=big[:, i, :], in_=ps[:, bk, :]),
                             R=[("ps", bk)], W=[("big", i)])
                    else:
                        P.op("act", lambda e, bk=bk, i=i: e.activation(out=big[:, i, :], in_=ps[:, bk, :], func=AF.Copy),
                             R=[("ps", bk)], W=[("big", i)])
                zf_x(prev)
                zf_kq(prev)
                wdone(l, 5)
                wdone(l, 7)

                P.mark("hgrn")
                hs = {}

                def st_S(b):
                    tok = slice(b * 128, (b + 1) * 128)
                    bS = [P.newbank(), P.newbank()]
                    for hd in range(8):
                        P.op("pe", lambda e, hd=hd, tok=tok, bS=bS: e.matmul(
                            ps[:, bS[hd // 4], (hd % 4) * 128:(hd % 4) * 128 + 128], lhsT=big[:, 8 + hd, tok], rhs=big[:, hd, tok],
                            start=True, stop=True), R=[("big", 8 + hd), ("big", hd)], W=[("ps", bS[hd // 4])])
                    bT = [P.newbank(), P.newbank()]
                    for hd in range(8):
                        P.op("pe", lambda e, hd=hd, tok=tok, bT=bT: e.matmul(
                            ps[:, bT[hd // 4], (hd % 4) * 128:(hd % 4) * 128 + 128], lhsT=big[:, 8 + hd, tok], rhs=identb[:],
                            start=True, stop=True), R=[("big", 8 + hd), ("identb",)], W=[("ps", bT[hd // 4])])
                    hs[b] = {"bS": bS, "bT": bT}

                def ev_K(b):
                    bT = hs[b]["bT"]
                    for j in range(2):
                        P.op("act", lambda e, j=j, bT=bT: e.activation(
                            out=ktok[:, j * 512:(j + 1) * 512], in_=ps[:, bT[j], :], func=AF.Copy),
                            R=[("ps", bT[j])], W=[("ktok", j)])

                def ev_P(b):
                    bS = hs[b]["bS"]
                    for j in range(2):
                        P.op("dve", lambda e, j=j, bS=bS, b=b: e.tensor_tensor(
                            out=PT[:, b % 2, j * 512:(j + 1) * 512], in0=ps[:, bS[j], :], in1=maskst[:], op=ALU.mult),
                            R=[("ps", bS[j]), ("maskst",)], W=[("PT", b % 2, j)])

                def st_D(b):
                    bD = [[P.newbank(), P.newbank()], [P.newbank(), P.newbank()]]
                    for ch in range(2):
                        rows = slice(ch * 64, ch * 64 + 64)
                        for hd in range(8):
                            P.op("pe", lambda e, hd=hd, ch=ch, rows=rows, b=b, bD=bD: e.matmul(
                                ps[:, bD[ch][hd // 4], (hd % 4) * 128:(hd % 4) * 128 + 128],
                                lhsT=ktok[rows, hd * 128:(hd + 1) * 128], rhs=vtok(b, hd, rows), start=True, stop=True),
                                R=[("ktok", hd // 4), vkey(b, hd)], W=[("ps", bD[ch][hd // 4])])
                    hs[b]["bD"] = bD

                def chain(b):
                    bD = hs[b]["bD"]
                    for ch in range(2):
                        gch = (ti * NB + b) * 2 + ch
                        if gch == 0:
                            P.op("dve", lambda e, ch=ch: e.memset(U[:, :, ch, :], 0.0), W=[("U", ch)])
                        else:
                            Ebc = Et[:, l, :, gch - 1:gch].to_broadcast([128, 8, 128])
                            P.op("dve", lambda e, ch=ch, l=l, Ebc=Ebc: e.tensor_tensor(
                                out=U[:, :, ch, :], in0=Tm[:, l, :, :], in1=Ebc, op=ALU.mult),
                                R=[("Tm", l), ("Et", l)], W=[("U", ch)])
                            P.op("dve", lambda e, l=l, Ebc=Ebc: e.tensor_tensor(
                                out=Tm[:, l, :, :], in0=Tm[:, l, :, :], in1=Ebc, op=ALU.mult),
                                R=[("Tm", l), ("Et", l)], W=[("Tm", l)])
                        for j in range(2):
                            P.op("dve", lambda e, ch=ch, l=l, j=j, bD=bD: e.tensor_tensor(
                                out=Tm[:, l, 4 * j:4 * j + 4, :], in0=Tm[:, l, 4 * j:4 * j + 4, :],
                                in1=ps[:, bD[ch][j], :].rearrange("p (h v) -> p h v", h=4), op=ALU.add),
                                R=[("Tm", l), ("ps", bD[ch][j])], W=[("Tm", l)])

                def st_O(b):
                    bO = [P.newbank(), P.newbank()]
                    for hd in range(8):
                        osl = lambda a, z, hd=hd, bO=bO: ps[:, bO[hd // 4], (hd % 4) * 128 + a:(hd % 4) * 128 + z]
                        P.op("pe", lambda e, hd=hd, b=b, osl=osl: e.matmul(
                            osl(0, 128), lhsT=vtok(b, hd), rhs=PT[:, b % 2, hd * 128:(hd + 1) * 128], start=True, stop=False),
                            R=[vkey(b, hd), ("PT", b % 2, hd // 4)], W=[("ps", bO[hd // 4])])
                        for ch in range(2):
                            P.op("pe", lambda e, hd=hd, b=b, ch=ch, osl=osl: e.matmul(
                                osl(ch * 64, ch * 64 + 64), lhsT=U[:, hd, ch, :],
                                rhs=big[:, hd, b * 128 + ch * 64:b * 128 + ch * 64 + 64], start=False, stop=(ch == 1)),
                                R=[("U", ch), ("big", hd)], W=[("ps", bO[hd // 4])])
                    hs[b]["bO"] = bO

                def ev_sq(b):
                    bO = hs[b]["bO"]
                    for j in range(2):
                        P.op("act", lambda e, j=j, bO=bO: e.activation(
                            out=osq[:, j * 512:(j + 1) * 512], in_=ps[:, bO[j], :], func=AF.Square),
                            R=[("ps", bO[j])], W=[("osq", j)])
                        P.op("act", lambda e, j=j, bO=bO, l=l: e.activation(
                            out=cf32[:, j, :], in_=ps[:, bO[j], :], func=AF.Copy, scale=hgg[:, l:l + 1]),
                            R=[("ps", bO[j]), ("hgg",)], W=[("cf", j)])

                def st_M(b):
                    tok = slice(b * 128, (b + 1) * 128)
                    bO = hs[b]["bO"]
                    bM = [P.newbank(), P.newbank()]
                    for j in range(2):
                        P.op("pe", lambda e, j=j, bM=bM: e.matmul(
                            ps[:, bM[j], :], lhsT=onesb[:], rhs=osq[:, j * 512:(j + 1) * 512], start=True, stop=True),
                            R=[("osq", j), ("onesb",)], W=[("ps", bM[j])])
                    for j in range(2):
                        P.op("act", lambda e, j=j, bM=bM: e.activation(
                            out=msvo[:, j, :], in_=ps[:, bM[j], :], func=AF.Ln, scale=1.0 / 128, bias=epsc[:, 0:1]),
                            R=[("ps", bM[j]), ("epsc",)], W=[("msvo", j)])
                        P.op("act", lambda e, j=j: e.activation(out=rso[:, j, :], in_=msvo[:, j, :], func=AF.Exp, scale=-0.5),
                             R=[("msvo", j)], W=[("rso", j)])
                        P.op("pool", lambda e, j=j: e.tensor_tensor(
                            out=otmp[:, j, :], in0=cf32[:, j, :], in1=rso[:, j, :], op=ALU.mult),
                            R=[("cf", j), ("rso", j)], W=[("otmp", j)])
                        P.op("pool", lambda e, j=j, tok=tok: e.tensor_tensor(
                            out=on[:, 4 * j:4 * j + 4, tok], in0=otmp[:, j, :].rearrange("p (h t) -> p h t", h=4),
                            in1=big[:, 24 + 4 * j:24 + 4 * j + 4, tok], op=ALU.mult),
                            R=[("otmp", j)] + [("big", 24 + 4 * j + i) for i in range(4)],
                            W=[("on", 4 * j + i) for i in range(4)])

                st_S(0)
                ev_K(0)
                ev_P(0)
                st_D(0)
                chain(0)
                for b in range(NB):
                    if b + 1 < NB:
                        st_S(b + 1)
                        ev_K(b + 1)
                        ev_P(b + 1)
                    st_O(b)
                    ev_sq(b)
                    if b + 1 < NB:
                        st_D(b + 1)
                        chain(b + 1)
                    gateA_chunk(2 * b)
                    st_M(b)
                    gateA_chunk(2 * b + 1)

                P.mark("gateB")
                gb_st = {}

                def gb_proj(c):
                    if c % 4 == 0:
                        if c:
                            wdone(l, 12)
                        gb_st["g"] = wget(l, 12 + c // 4)
                    bkg = proj_fm(gb_st["g"], (c % 4) * 128, 8, 512, hb_rhs, hb_key)
                    s_ = c % 2
                    P.op("act", lambda e, bkg=bkg, s_=s_: e.activation(out=sgr[:, s_, :], in_=ps[:, bkg, :], func=AF.Sigmoid),
                         R=[("ps", bkg)], W=[("sgr", s_)])

                gb_proj(0)
                for c in range(8):
                    if c + 1 < 8:
                        gb_proj(c + 1)
                    if c % 4 == 0:
                        if c:
                            wdone(l, G_UPB)
                        gb_st["b"] = wget(l, G_UPB + c // 4)
                    s_ = c % 2
                    bky = proj_fm(gb_st["b"], (c % 4) * 128, 8, 512, lambda kc: on[:, kc, :], lambda kc: ("on", kc))
                    P.op("dve", lambda e, bky=bky, s_=s_: e.tensor_tensor(
                        out=mt[:, s_, :], in0=ps[:, bky, :], in1=sgr[:, s_, :], op=ALU.mult),
                        R=[("ps", bky), ("sgr", s_)], W=[("mt", s_)])
                    P.op("pool", lambda e, s_=s_, c=c: e.tensor_tensor(
                        out=mixA[:, c, :], in0=mt[:, s_, :], in1=mixA[:, c, :], op=ALU.add),
                        R=[("mt", s_), ("mixA", c)], W=[("mixA", c)])
                wdone(l, 13)
                wdone(l, G_UPB + 1)
                P.mark("wout")
                for c in range(8):
                    if c % 4 == 0:
                        if c:
                            wdone(l, G_OUT)
                        slot = wget(l, G_OUT + c // 4)
                    if c == 0:
                        stats_begin()
                    bk = proj_fm(slot, (c % 4) * 128, 8, 512, lambda kc: mixA[:, kc, :], lambda kc: ("mixA", kc))
                    if c:
                        stats_mm(c - 1)
                    P.op("dve", lambda e, bk=bk, c=c: e.tensor_tensor(out=x[:, c, :], in0=ps[:, bk, :], in1=x[:, c, :], op=ALU.add),
                         R=[("ps", bk), ("x", c)], W=[("x", c)])
                    stats_sq(c)
                stats_mm(7)

                wdone(l, G_OUT + 1)
                P.mark("ffn1")
                rmsnorm_to_hb(16 + l * 8, have_stats=True)
                for c in range(8):
                    norm_apply(16 + l * 8, c, hb[:, c, :], [("hb", c)])
                for j in range(32):
                    if j % 4 == 0:
                        if j:
                            wdone(l, G_FF1 + j // 4 - 1)
                        slot = wget(l, G_FF1 + j // 4)
                    bk = proj_fm(slot, (j % 4) * 128, 8, 512, hb_rhs, hb_key)
                    tr = P.newtmp()
                    P.op("act", lambda e, bk=bk, tr=tr: e.activation(out=tmp[:, tr, :], in_=ps[:, bk, :], func=AF.Relu),
                         R=[("ps", bk)], W=[("tmp", tr)])
                    P.op("pool", lambda e, tr=tr, j=j: e.tensor_tensor(out=big[:, j, :], in0=tmp[:, tr, :], in1=tmp[:, tr, :],
                                                                        op=ALU.mult), R=[("tmp", tr)], W=[("big", j)])
                wdone(l, G_FF1 + 7)
                P.mark("ffn2")
                for c in range(8):
                    if c:
                        wdone(l, G_FF2 + c - 1)
                    slot = wget(l, G_FF2 + c)
                    if c == 0:
                        stats_begin()
                    bk = proj_fm(slot, 0, 32, 128, lambda kc: big[:, kc, :], lambda kc: ("big", kc))
                    if c:
                        stats_mm(c - 1)
                    P.op("dve", lambda e, bk=bk, c=c: e.tensor_tensor(out=x[:, c, :], in0=ps[:, bk, :], in1=x[:, c, :], op=ALU.add),
                         R=[("ps", bk), ("x", c)], W=[("x", c)])
                    stats_sq(c)
                stats_mm(7)
                wdone(l, G_FF2 + 7)

            P.mark("final")
            rmsnorm_to_hb(32, have_stats=True)
            for c in range(8):
                to = P.newtmp()
                norm_apply(32, c, tmp[:, to, :], [("tmp", to)])
                P.dma(lambda e, to=to, c=c, tsl=tsl: e.dma_start(out=yT[c * 128:(c + 1) * 128, tsl], in_=tmp[:, to, :]),
                      "o%d" % to, R=[("tmp", to)], W=[("yT", ti, c)])

        semnames = list(Prog.ENGS) + ["dma_" + s for s in P.dmacnt]
        sems = {n: es.enter_context(nc.semaphore(n)) for n in semnames}
        block = es.enter_context(nc.Block())
        P.emit(nc, block, sems)
    nc._marks = P.marks
    return nc


def _host_params(gmlp_ln_g, gmlp_ln_b, gmlp_ws, gmlp_bs, hg_lb, hg_norm_g, norm_mix, norm_ffn, final_norm):
    f = np.float32
    gn = np.zeros((128, 40), f)
    for l in range(2):
        gn[:, l * 8:(l + 1) * 8] = np.asarray(norm_mix[l], f).reshape(8, 128).T
        gn[:, 16 + l * 8:16 + (l + 1) * 8] = np.asarray(norm_ffn[l], f).reshape(8, 128).T
    gn[:, 32:40] = np.asarray(final_norm, f).reshape(8, 128).T
    lng = np.ascontiguousarray(np.broadcast_to(np.asarray(gmlp_ln_g, f)[None], (128, 2, 512)))
    lnb = np.ascontiguousarray(np.broadcast_to(np.asarray(gmlp_ln_b, f)[None], (128, 2, 512)))
    wst = np.ascontiguousarray(np.asarray(gmlp_ws, f).transpose(3, 0, 1, 2).reshape(128, 1024))
    bsr = np.ascontiguousarray(np.asarray(gmlp_bs, f).reshape(1, 1024))
    hglb = np.ascontiguousarray(np.asarray(hg_lb, f).reshape(2, 8, 128).transpose(2, 0, 1))
    hgg = np.ascontiguousarray(np.asarray(hg_norm_g, f).T)
    ident = np.eye(128, dtype=f)
    s = np.arange(128)[:, None]
    t = np.arange(128)[None, :]
    m = ((s // 64 == t // 64) & (s <= t)).astype(f)
    maskst = np.ascontiguousarray(np.tile(m, (1, 4)))
    scanm = np.ones((128, 512), f)
    scanm[:, ::64] = 0.0
    return dict(gn=gn, lng=lng, lnb=lnb, wst=wst, bsr=bsr, hglb=hglb, hgg=hgg, ident=ident, maskst=maskst, scanm=scanm)


_NC_CACHE = {}


def _run(x, w_in, gmlp_ln_g, gmlp_ln_b, gmlp_ws, gmlp_bs, w_up_a, hg_lb, hg_norm_g, w_up_b, w_out, norm_mix, norm_ffn,
         w_ff1, w_ff2, final_norm, NL=2, trace=False):
    x = np.asarray(x, np.float32)
    B, S, _ = x.shape
    NT = S // T
    key = (NT, NL)
    if key not in _NC_CACHE:
        _NC_CACHE[key] = build_nc(NT, NL)
    nc = _NC_CACHE[key]
    shared = _host_params(gmlp_ln_g, gmlp_ln_b, gmlp_ws, gmlp_bs, hg_lb, hg_norm_g, norm_mix, norm_ffn, final_norm)
    c = lambda a: np.ascontiguousarray(np.asarray(a, np.float32))
    shared.update(w_in=c(w_in), w_up_a=c(w_up_a), w_up_b=c(w_up_b), w_out=c(w_out), w_ff1=c(w_ff1), w_ff2=c(w_ff2))
    in_maps = []
    for b in range(B):
        m = dict(shared)
        m["xT"] = np.ascontiguousarray(x[b].T)
        in_maps.append(m)
    res = run_bass_kernel_spmd(nc, in_maps, core_ids=list(range(B)), **({"trace": True} if trace else {}))
    out = np.stack([np.ascontiguousarray(r["yT"].T) for r in res.results], axis=0)
    return out.astype(np.float32), res


def kernel(x, w_in, gmlp_ln_g, gmlp_ln_b, gmlp_ws, gmlp_bs, w_up_a, hg_lb, hg_norm_g, w_up_b, w_out, norm_mix, norm_ffn,
           w_ff1, w_ff2, final_norm):
    out, _ = _run(x, w_in, gmlp_ln_g, gmlp_ln_b, gmlp_ws, gmlp_bs, w_up_a, hg_lb, hg_norm_g, w_up_b, w_out, norm_mix,
                  norm_ffn, w_ff1, w_ff2, final_norm)
    return out
```
